# Optimizing a Trainium2 kernel written in Bass

```python
import jax, jax.numpy as jnp
from jax import lax
import numpy as np

D_MODEL = 1024
BATCH = 16
SEQ = 2048
DEPTH = 2

EPS = 1e-6
N_BRANCH = 3
BRANCH_WIDTH = D_MODEL // 2
GLA_HEADS = 4
GLA_DV = BRANCH_WIDTH // GLA_HEADS
GLA_DK = GLA_DV // 2
GLA_RANK = 16
GLA_TAU = 16.0
GLA_CHUNK = 16
GLA_QK = GLA_HEADS * GLA_DK
GLA_V = GLA_HEADS * GLA_DV
SC_WIDTH = BRANCH_WIDTH
CONV_K = 3
DIL_PATTERNS = ((128, 1), (512, 4), (2048, 16))
DIL_GROUPS = len(DIL_PATTERNS)
DIL_HEADS = 4
DIL_HD = BRANCH_WIDTH // DIL_HEADS
DIL_W = DIL_GROUPS * DIL_HEADS * DIL_HD
DIL_BLOCK = 128
ROPE_THETA = 10000.0
D_FF = 2816

IN_SIZES = (GLA_QK, GLA_QK, GLA_V, GLA_V, GLA_RANK,
            SC_WIDTH, SC_WIDTH, SC_WIDTH,
            DIL_W, DIL_W, DIL_W)
IN_COLS = sum(IN_SIZES)

kernel_name = "hybrid_gated_gla_shortconv_dilated_attn"


def rmsnorm(x, g):
    xf = x.astype(jnp.float32)
    r = lax.rsqrt(jnp.mean(xf * xf, axis=-1, keepdims=True) + EPS)
    return (xf * r).astype(x.dtype) * g


def causal_dwconv(u, w):
    K = w.shape[0]
    S = u.shape[1]
    up = jnp.pad(u, ((0, 0), (K - 1, 0), (0, 0)))
    y = w[0] * up[:, 0:S]
    for k in range(1, K):
        y = y + w[k] * up[:, k:k + S]
    return y


def rope(x, pos):
    hd = x.shape[-1]
    inv = ROPE_THETA ** (-jnp.arange(0, hd, 2, dtype=jnp.float32) / hd)
    ang = pos.astype(jnp.float32)[..., None] * inv
    cos = jnp.cos(ang)[:, :, None, None, :]
    sin = jnp.sin(ang)[:, :, None, None, :]
    xf = x.astype(jnp.float32)
    x1, x2 = xf[..., :hd // 2], xf[..., hd // 2:]
    return jnp.concatenate([x1 * cos - x2 * sin, x2 * cos + x1 * sin], axis=-1).astype(x.dtype)


def gla_branch(q, k, v, g, a_low, w_a_up, b_a, norm_g):
    B, S, _ = q.shape
    H, C = GLA_HEADS, GLA_CHUNK
    N = S // C
    f32 = jnp.float32
    loga = jax.nn.log_sigmoid((a_low.astype(f32) @ w_a_up.astype(f32)) + b_a.astype(f32)) / GLA_TAU

    def chunks(t, d):
        return t.astype(f32).reshape(B, N, C, H, d).transpose(0, 3, 1, 2, 4)

    qc = chunks(q, GLA_DK) * (GLA_DK ** -0.5)
    kc = chunks(k, GLA_DK)
    vc = chunks(v, GLA_DV)
    lc = jnp.cumsum(chunks(loga, GLA_DK), axis=3)

    causal = jnp.tril(jnp.ones((C, C), dtype=bool))
    rel = lc[..., :, None, :] - lc[..., None, :, :]
    decay = jnp.exp(jnp.where(causal[:, :, None], rel, -jnp.inf))
    scores = jnp.einsum('bhnid,bhnjd,bhnijd->bhnij', qc, kc, decay)
    o_intra = jnp.einsum('bhnij,bhnje->bhnie', scores, vc)

    lend = lc[..., -1:, :]
    q_in = qc * jnp.exp(lc)
    k_in = kc * jnp.exp(lend - lc)
    dec_chunk = jnp.exp(lend[..., 0, :])

    def step(state, xs):
        qn, kn, vn, dn = xs
        o = jnp.einsum('bhid,bhde->bhie', qn, state)
        state = state * dn[..., None] + jnp.einsum('bhjd,bhje->bhde', kn, vn)
        return state, o

    xs = (jnp.moveaxis(q_in, 2, 0), jnp.moveaxis(k_in, 2, 0),
          jnp.moveaxis(vc, 2, 0), jnp.moveaxis(dec_chunk, 2, 0))
    s0 = jnp.zeros((B, H, GLA_DK, GLA_DV), f32)
    _, o_inter = lax.scan(step, s0, xs)
    o = o_intra + jnp.moveaxis(o_inter, 0, 2)
    o = o.transpose(0, 2, 3, 1, 4).reshape(B, S, H, GLA_DV).astype(v.dtype)
    o = rmsnorm(o, norm_g).reshape(B, S, GLA_V)
    return o * jax.nn.silu(g)


def strided_window_attn(q, k, v, span, dil):
    B, S, H, D = q.shape
    L = S // dil
    BLK = DIL_BLOCK
    nb = -(-L // BLK)
    Lp = nb * BLK

    def to_sub(t):
        return t.reshape(B, L, dil, H, D).transpose(0, 2, 3, 1, 4)

    qs, ks, vs = to_sub(q), to_sub(k), to_sub(v)
    qb = jnp.pad(qs, ((0, 0), (0, 0), (0, 0), (0, Lp - L), (0, 0))).reshape(B, dil, H, nb, BLK, D)

    def kv_blocks(t):
        tp = jnp.pad(t, ((0, 0), (0, 0), (0, 0), (BLK, Lp - L), (0, 0)))
        prev = tp[:, :, :, :Lp].reshape(B, dil, H, nb, BLK, D)
        cur = tp[:, :, :, BLK:].reshape(B, dil, H, nb, BLK, D)
        return jnp.concatenate([prev, cur], axis=4)

    kb, vb = kv_blocks(ks), kv_blocks(vs)
    i = jnp.arange(BLK)[:, None]
    m = jnp.arange(2 * BLK)[None, :]
    blk = jnp.arange(nb)[:, None, None]
    dist = i + BLK - m
    key_idx = (blk - 1) * BLK + m
    valid = (dist >= 0) & (dist <= span) & (key_idx >= 0)

    s = jnp.einsum('brhnid,brhnmd->brhnim', qb, kb).astype(jnp.float32) * (D ** -0.5)
    s = jnp.where(valid, s, -jnp.inf)
    mx = jnp.max(s, axis=-1, keepdims=True)
    p = jnp.exp(s - mx)
    den = jnp.sum(p, axis=-1, keepdims=True)
    o = jnp.einsum('brhnim,brhnmd->brhnid', p, vb.astype(jnp.float32)) / den
    lse = (mx + jnp.log(den))[..., 0]

    o = o.reshape(B, dil, H, Lp, D)[:, :, :, :L].transpose(0, 3, 1, 2, 4).reshape(B, S, H, D)
    lse = lse.reshape(B, dil, H, Lp)[:, :, :, :L].transpose(0, 3, 1, 2).reshape(B, S, H)
    return o, lse


def dilated_branch(dq, dk, dv, pos):
    B, S, _ = dq.shape
    shp = (B, S, DIL_GROUPS, DIL_HEADS, DIL_HD)
    q = rope(dq.reshape(shp), pos)
    k = rope(dk.reshape(shp), pos)
    v = dv.reshape(shp)
    outs, lses = [], []
    for gi, (win, dil) in enumerate(DIL_PATTERNS):
        o, lse = strided_window_attn(q[:, :, gi], k[:, :, gi], v[:, :, gi], win // dil, dil)
        outs.append(o)
        lses.append(lse)
    w = jax.nn.softmax(jnp.stack(lses, axis=0), axis=0)
    o = jnp.sum(w[..., None] * jnp.stack(outs, axis=0), axis=0)
    return o.reshape(B, S, DIL_HEADS * DIL_HD).astype(dq.dtype)


def setup_inputs(seed: int = 0) -> dict:
    key = jax.random.key(seed)
    ks = jax.random.split(key, 24)
    f32 = jnp.float32

    def nrm(k, shape, scale):
        return jax.random.normal(k, shape, f32) * scale

    def gain(k, shape):
        return 1.0 + 0.02 * jax.random.normal(k, shape, f32)

    x = jax.random.normal(ks[0], (BATCH, SEQ, D_MODEL), f32)
    offset = jax.random.randint(ks[1], (BATCH, 1), 0, 4096, dtype=jnp.int32)
    positions = (offset + jnp.arange(SEQ, dtype=jnp.int32)[None, :]).astype(jnp.int32)
    return {
        "x": x,
        "positions": positions,
        "w_in": nrm(ks[2], (DEPTH, D_MODEL, IN_COLS), D_MODEL ** -0.5),
        "w_alpha_up": nrm(ks[3], (DEPTH, GLA_RANK, GLA_QK), GLA_RANK ** -0.5),
        "b_alpha": nrm(ks[4], (DEPTH, GLA_QK), 0.1),
        "gla_norm_g": gain(ks[5], (DEPTH, GLA_DV)),
        "sc_conv_w": nrm(ks[6], (DEPTH, CONV_K, SC_WIDTH), CONV_K ** -0.5),
        "w_gate": nrm(ks[7], (DEPTH, D_MODEL, N_BRANCH * D_MODEL), D_MODEL ** -0.5),
        "b_gate": nrm(ks[8], (DEPTH, N_BRANCH * D_MODEL), 0.02),
        "w_branch": nrm(ks[9], (DEPTH, N_BRANCH, BRANCH_WIDTH, D_MODEL), BRANCH_WIDTH ** -0.5),
        "w_mix_out": nrm(ks[10], (DEPTH, D_MODEL, D_MODEL), D_MODEL ** -0.5),
        "pre_mix_g": gain(ks[11], (DEPTH, D_MODEL)),
        "post_mix_g": gain(ks[12], (DEPTH, D_MODEL)),
        "pre_ffn_g": gain(ks[13], (DEPTH, D_MODEL)),
        "post_ffn_g": gain(ks[14], (DEPTH, D_MODEL)),
        "w_ff_gate": nrm(ks[15], (DEPTH, D_MODEL, D_FF), D_MODEL ** -0.5),
        "w_ff_up": nrm(ks[16], (DEPTH, D_MODEL, D_FF), D_MODEL ** -0.5),
        "ff_conv_w": nrm(ks[17], (DEPTH, CONV_K, D_FF), CONV_K ** -0.5),
        "ff_conv_b": nrm(ks[18], (DEPTH, D_FF), 0.02),
        "w_ff_down": nrm(ks[19], (DEPTH, D_FF, D_MODEL), D_FF ** -0.5),
    }


def reference(x, positions, w_in, w_alpha_up, b_alpha, gla_norm_g, sc_conv_w, w_gate, b_gate,
              w_branch, w_mix_out, pre_mix_g, post_mix_g, pre_ffn_g, post_ffn_g,
              w_ff_gate, w_ff_up, ff_conv_w, ff_conv_b, w_ff_down):
    B, S, _ = x.shape
    split_points = np.cumsum(np.array(IN_SIZES))[:-1].tolist()
    for l in range(DEPTH):
        h = rmsnorm(x, pre_mix_g[l])
        z = h @ w_in[l]
        (g_q, g_k, g_v, g_o, a_low, sc_b, sc_c, sc_x,
         d_q, d_k, d_v) = jnp.split(z, split_points, axis=-1)

        y_gla = gla_branch(g_q, g_k, g_v, g_o, a_low, w_alpha_up[l], b_alpha[l], gla_norm_g[l])
        y_sc = sc_b * causal_dwconv(sc_c * sc_x, sc_conv_w[l])
        y_dil = dilated_branch(d_q, d_k, d_v, positions)

        br = jnp.stack([y_gla, y_sc, y_dil], axis=2)
        proj = jnp.einsum('bsgc,gcd->bsgd', br, w_branch[l])
        gates = jax.nn.sigmoid(h @ w_gate[l] + b_gate[l]).reshape(B, S, N_BRANCH, D_MODEL)
        merged = jnp.sum(gates * proj, axis=2)
        x = x + rmsnorm(merged @ w_mix_out[l], post_mix_g[l])

        h = rmsnorm(x, pre_ffn_g[l])
        gt = causal_dwconv(h @ w_ff_gate[l], ff_conv_w[l]) + ff_conv_b[l]
        y = (jax.nn.gelu(gt, approximate=True) * (h @ w_ff_up[l])) @ w_ff_down[l]
        x = x + rmsnorm(y, post_ffn_g[l])
    return x
```

```python
import math
from contextlib import ExitStack
import numpy as np
import concourse.bass as bass
import concourse.mybir as mybir
from concourse.bass_utils import run_bass_kernel_spmd

F32 = mybir.dt.float32
BF16 = mybir.dt.bfloat16
I32 = mybir.dt.int32
AF = mybir.ActivationFunctionType
ALU = mybir.AluOpType

PHASE_LOG = []
ENGS = ["pe", "act", "dve", "pool", "sp"]
N_DMA_SEMS = 8

S = 2048
D = 1024
NT = 4
DFF = 2816
NFF = 22
EPS = 1e-6
DILS = (1, 4, 16)
import os
GSET = [int(c) for c in os.environ.get('ATT_G', '012')]
ATT_LVL = int(os.environ.get('ATT_LVL', '9'))
ROPE_ADD_ENG = os.environ.get('ROPE_ADD_ENG', 'pool')
ATT_SUB = int(os.environ.get('ATT_SUB', '9'))
RES_ENG = os.environ.get('RES_ENG', 'dve')
MIX_SUB = int(os.environ.get('MIX_SUB', '9'))
O_GQ, O_GK, O_GV, O_GO, O_AL, O_SB, O_SC, O_SX, O_DQ, O_DK, O_DV = (
    0, 256, 512, 1024, 1536, 1552, 2064, 2576, 3088, 4624, 6160)

W_ATT = 0
W_GLA = W_ATT + 12 * 3072
W_ALOW = W_GLA + 4 * 3072
W_SC = W_ALOW + 256
W_MRG = W_SC + 4 * 3072
W_MIX = W_MRG + 24 * 1536
W_FF = W_MIX + 8192
W_DN = W_FF + NFF * 2048
W_TOT = W_DN + 8 * 2816

V_PRE_MIX, V_POST_MIX, V_PRE_FFN, V_POST_FFN = 0, 8, 16, 24
V_BGATE = 32
V_FFCW = 56
V_FFCB = 122
V_SCCW = 144
V_GLAG = 156
NV = 160

C_ID, C_TRIL, C_UPP, C_MASK, C_RMAT, C_INV, C_MBIAS = 0, 128, 256, 384, 640, 768, 772
NCONST = 772 + 256


class Buf:
    __slots__ = ("w", "rs", "rd")

    def __init__(self):
        self.w = None
        self.rs = {}
        self.rd = []


class Op:
    __slots__ = ("eng", "idx", "fn", "deps", "waited", "seq", "dma", "dsem", "dval", "needed")


class Prog:
    def __init__(self):
        self.ops = {e: [] for e in ENGS}
        self.dma_cnt = {}
        self.dma_last = {}
        self.dma_rr = {e: 0 for e in ENGS}
        self.last_real = {}

    def add(self, eng, fn, reads=(), writes=(), dma=False):
        op = Op()
        op.eng = eng
        op.idx = len(self.ops[eng])
        op.fn = fn
        op.dma = dma
        op.waited = False
        op.seq = 0
        op.needed = []
        deps = []
        for b in reads:
            if b.w is not None:
                deps.append((b.w, "raw"))
        for b in writes:
            if b.w is not None:
                deps.append((b.w, "waw"))
            for r in b.rs.values():
                deps.append((r, "war"))
            for r in b.rd:
                deps.append((r, "war"))
        if dma:
            k = self.dma_rr[eng] % N_DMA_SEMS
            self.dma_rr[eng] += 1
            key = (eng, k)
            prev = self.dma_last.get(key)
            if prev is not None:
                deps.append((prev, "sem"))
            self.dma_cnt[key] = self.dma_cnt.get(key, 0) + 1
            op.dsem = key
            op.dval = 16 * self.dma_cnt[key]
            self.dma_last[key] = op
        for b in reads:
            if dma:
                b.rd.append(op)
            else:
                b.rs[eng] = op
        for b in writes:
            b.w = op
            b.rs = {}
            b.rd = []
        op.deps = deps
        self.ops[eng].append(op)
        if fn is not None and not dma:
            self.last_real[eng] = op
        return op

    def barrier(self):
        lasts = list(self.last_real.values()) + list(self.dma_last.values())
        for e in ENGS:
            op = self.add(e, None)
            op.deps = [(p, "raw") for p in lasts]

    def finalize(self):
        for eng in ENGS:
            known = {}
            for op in self.ops[eng]:
                need = {}
                for (p, kind) in op.deps:
                    if p.dma:
                        key = ("dma",) + p.dsem
                        val = p.dval
                    else:
                        if p.eng == eng and eng == "pe":
                            continue
                        key = p.eng
                        val = p.idx + 1
                    if known.get(key, 0) >= val:
                        continue
                    if key not in need or need[key][0] < val:
                        need[key] = (val, p)
                for key, (val, p) in need.items():
                    known[key] = val
                    p.waited = True
                    op.needed.append(p)
        for eng in ENGS:
            n = 0
            for op in self.ops[eng]:
                if (not op.dma) and op.waited:
                    n += 1
                    op.seq = n

    def emit(self, nc, stack):
        self.finalize()
        esem = {e: stack.enter_context(nc.semaphore("s_" + e)) for e in ENGS}
        dsem = {}
        for key in self.dma_cnt:
            dsem[key] = stack.enter_context(nc.semaphore("d_%s%d" % key))
        block = stack.enter_context(nc.Block())

        def run(eng_name):
            def body(e):
                for op in self.ops[eng_name]:
                    for p in op.needed:
                        if p.dma:
                            e.wait_ge(dsem[p.dsem], p.dval)
                        else:
                            e.wait_ge(esem[p.eng], p.seq)
                    if op.fn is None:
                        continue
                    ins = op.fn(e)
                    if op.dma:
                        ins.then_inc(dsem[op.dsem], 16)
                    elif op.waited:
                        ins.then_inc(esem[eng_name], 1)
            return body

        block.tensor(run("pe"))
        block.scalar(run("act"))
        block.vector(run("dve"))
        block.gpsimd(run("pool"))
        block.sync(run("sp"))
        return {e: (len(self.ops[e]), sum(len(o.needed) for o in self.ops[e])) for e in ENGS}


class T:
    __slots__ = ("ap", "bufs")

    def __init__(self, ap, bufs):
        self.ap = ap
        self.bufs = list(bufs) if isinstance(bufs, (list, tuple)) else [bufs]


def build(nseq=2, nlayers=2, taps=None, stop=99):
    taps = taps or {}
    nc = bass.Bass("TRN2", target_bir_lowering=False)
    x_d = nc.dram_tensor("x", [nseq, S, D], F32, kind="ExternalInput").ap()
    pos_d = nc.dram_tensor("posb", [nseq, 128, S], I32, kind="ExternalInput").ap()
    wpk_d = nc.dram_tensor("wpk", [2, 128, W_TOT], F32, kind="ExternalInput").ap()
    vec_d = nc.dram_tensor("vecs", [2, 128, NV], F32, kind="ExternalInput").ap()
    wup_d = nc.dram_tensor("wup", [2, 32, 512], F32, kind="ExternalInput").ap()
    cst_d = nc.dram_tensor("consts", [128, NCONST], F32, kind="ExternalInput").ap()
    out_d = nc.dram_tensor("out", [nseq, S, D], F32, kind="ExternalOutput").ap()
    tap_d = {}
    for name, shape in taps.items():
        tap_d[name] = nc.dram_tensor("tap_" + name, list(shape), F32, kind="ExternalOutput").ap()

    P = Prog()
    st = ExitStack()
    skip = [False]
    _orig_add = P.add

    def _add(eng, fn, reads=(), writes=(), dma=False):
        if skip[0]:
            return Op()
        return _orig_add(eng, fn, reads, writes, dma)
    P.add = _add

    def phase(k):
        skip[0] = (k > stop)
        PHASE_LOG.append((k, len(P.ops["pe"])))

    def sb(name, shape, dt):
        return st.enter_context(nc.sbuf_tensor(name, list(shape), dt))

    xT = sb("xT", [128, 8, S], F32)
    xT_b = [[Buf() for _ in range(NT)] for _ in range(8)]
    cosT = sb("cosT", [128, S], BF16)
    sinT = sb("sinT", [128, S], BF16)
    tab_b = Buf()
    ident = sb("ident", [128, 128], F32)
    trilp = sb("trilp", [128, 128], F32)
    upp = sb("upp", [128, 128], F32)
    inv2 = sb("inv2", [128, 1], F32)
    maskA = sb("maskA", [128, 256], BF16)
    rmat = sb("rmat", [128, 128], BF16)
    ident_bf = sb("ident_bf", [128, 128], BF16)
    mbias = sb("mbias", [128, 256], BF16)
    ones_bf = sb("ones_bf", [128, 128], BF16)
    ones32 = sb("ones32", [32, 128], F32)
    epsT = sb("epsT", [128, 1], F32)
    vecs = sb("vecs_sb", [128, 2, NV], F32)
    cst_b = Buf()
    wst = [sb("wst%d" % i, [128, 3072], BF16) for i in range(2)]
    wst_b = [Buf(), Buf()]
    wst_rr = [0]
    RN = 59648
    posi_t = sb("posi", [128, 512], I32)
    posi_b = Buf()
    R = sb("R", [128, RN], BF16)
    print("sbuf bytes remaining", nc.sbuf_bytes_remaining)

    ps = [st.enter_context(nc.psum_tensor("ps%d" % i, [128, 512], F32)) for i in range(8)]
    ps_b = [Buf() for _ in range(8)]
    pool_banks = {"a": list(range(8)), "b": []}
    ps_rr = {"a": 0, "b": 0}

    def set_pool(banks, which="a"):
        pool_banks[which] = list(banks)
        ps_rr[which] = 0

    def bank(which="a"):
        k = pool_banks[which][ps_rr[which] % len(pool_banks[which])]
        ps_rr[which] += 1
        return T(ps[k][:, :], ps_b[k])

    def bufs_of(*ts):
        out = []
        for t in ts:
            if t is None or not isinstance(t, T):
                continue
            out.extend(t.bufs)
        return out

    def apof(t):
        return t.ap if isinstance(t, T) else t

    def sub(t, ap):
        return T(ap, t.bufs)

    def mm(out, lhsT, rhs, start=True, stop=True):
        o, l, r = out.ap, lhsT.ap, rhs.ap
        P.add("pe", lambda e: e.matmul(o, lhsT=l, rhs=r, start=start, stop=stop),
              reads=bufs_of(lhsT, rhs), writes=bufs_of(out))

    def tr(out, in_):
        o, i = out.ap, in_.ap
        P.add("pe", lambda e: e.transpose(o, i, ident[:, :]), reads=bufs_of(in_) + [cst_b], writes=bufs_of(out))

    def act(out, in_, func, scale=1.0, bias=None, extra_reads=()):
        o, i = out.ap, in_.ap
        kw = {}
        if bias is not None:
            kw["bias"] = apof(bias)
        P.add("act", lambda e: e.activation(out=o, in_=i, func=func, scale=scale, **kw),
              reads=bufs_of(in_, bias) + list(extra_reads), writes=bufs_of(out))

    def cp(eng, out, in_):
        o, i = out.ap, in_.ap
        if eng == "act":
            P.add("act", lambda e: e.copy(out=o, in_=i), reads=bufs_of(in_), writes=bufs_of(out))
        else:
            P.add(eng, lambda e: e.tensor_copy(out=o, in_=i), reads=bufs_of(in_), writes=bufs_of(out))

    def tt(eng, out, in0, in1, op):
        o, a, b = out.ap, in0.ap, in1.ap
        P.add(eng, lambda e: e.tensor_tensor(out=o, in0=a, in1=b, op=op),
              reads=bufs_of(in0, in1), writes=bufs_of(out))

    def ts(eng, out, in0, s1, s2, op0, op1=None):
        o, a = out.ap, in0.ap
        a1, a2 = apof(s1), apof(s2)
        if op1 is None:
            P.add(eng, lambda e: e.tensor_scalar(out=o, in0=a, scalar1=a1, scalar2=None, op0=op0),
                  reads=bufs_of(in0, s1), writes=bufs_of(out))
        else:
            P.add(eng, lambda e: e.tensor_scalar(out=o, in0=a, scalar1=a1, scalar2=a2, op0=op0, op1=op1),
                  reads=bufs_of(in0, s1, s2), writes=bufs_of(out))

    def stt(eng, out, in0, scalar, in1, op0, op1):
        o, a, b = out.ap, in0.ap, in1.ap
        sc = apof(scalar)
        P.add(eng, lambda e: e.scalar_tensor_tensor(out=o, in0=a, scalar=sc, in1=b, op0=op0, op1=op1),
              reads=bufs_of(in0, scalar, in1), writes=bufs_of(out))

    def memset(eng, out, val):
        o = out.ap
        P.add(eng, lambda e: e.memset(o, val), writes=bufs_of(out))

    def recip(out, in_):
        o, i = out.ap, in_.ap
        P.add("dve", lambda e: e.reciprocal(out=o, in_=i), reads=bufs_of(in_), writes=bufs_of(out))

    def dma(eng, out, in_, reads=(), writes=(), slow=False):
        if slow:
            P.add(eng, lambda e: e.dma_start(out=out, in_=in_, allow_slow_non_contiguous=True),
                  reads=list(reads), writes=list(writes), dma=True)
        else:
            P.add(eng, lambda e: e.dma_start(out=out, in_=in_), reads=list(reads), writes=list(writes), dma=True)

    def tap(name, src):
        if name in tap_d:
            d = tap_d[name]
            dma("pool", d, src.ap, reads=src.bufs)

    def V(l, off, n=1):
        return T(vecs[:, l, off:off + n], cst_b)

    class Region:
        def __init__(self):
            self.off = 0

        def reset(self, off=0):
            self.off = off

        def bf(self, n):
            a = self.off
            self.off += n
            assert self.off <= RN, ("region overflow", self.off, RN)
            return R[:, a:a + n]

        def f32(self, n):
            if self.off % 2:
                self.off += 1
            a = self.off
            self.off += 2 * n
            assert self.off <= RN, ("region overflow", self.off, RN)
            return R[:, a:a + 2 * n].bitcast(F32)

        def i32(self, n):
            if self.off % 2:
                self.off += 1
            a = self.off
            self.off += 2 * n
            assert self.off <= RN
            return R[:, a:a + 2 * n].bitcast(I32)

    reg = Region()

    class Ring:
        def __init__(self, aps):
            self.items = [T(a, Buf()) for a in aps]
            self.i = 0

        def next(self):
            t = self.items[self.i % len(self.items)]
            self.i += 1
            return t

    pend = [None]

    def _mk_dma(k):
        cell = {}
        op = P.add("pool", lambda e: e.dma_start(out=cell["out"], in_=cell["in"]), writes=[wst_b[k]], dma=True)
        return cell

    def wload(l, off, n):
        k = wst_rr[0] % 2
        wst_rr[0] += 1
        if skip[0]:
            return k
        if pend[0] is None:
            cell = _mk_dma(k)
        else:
            cell = pend[0]
        cell["out"] = wst[k][:, 0:n]
        cell["in"] = wpk_d[l, :, off:off + n]
        pend[0] = _mk_dma((k + 1) % 2)
        return k

    def wflush():
        if pend[0] is not None:
            k = wst_rr[0] % 2
            pend[0]["out"] = wst[k][:, 0:16]
            pend[0]["in"] = wpk_d[0, :, 0:16]
            pend[0] = None

    def wblk(k, off, kc, w):
        return T(wst[k][:, off:off + kc * w].rearrange("p (c w) -> p c w", c=kc), wst_b[k])

    def perm(ap, d, n0, w):
        L = S // d
        if d == 1:
            return ap[:, n0:n0 + w], 1
        v = ap.rearrange("p (l r) -> p r l", r=d)
        if w <= L:
            r = n0 // L
            l0 = n0 % L
            return v[:, r, l0:l0 + w], 1
        k = w // L
        r0 = n0 // L
        return v[:, r0:r0 + k, :], k

    def permw(ap, d, i):
        if d == 1:
            return ap[:, i * 512:(i + 1) * 512]
        n = 512 // d
        return ap.rearrange("p (r l) -> p l r", r=d)[:, i * n:(i + 1) * n, :]

    def shp2(ap, d):
        if d == 1:
            return ap
        return ap.rearrange("p (l r) -> p l r", r=d)

    def shp(ap, k):
        if k == 1:
            return ap
        return ap.rearrange("p (k l) -> p k l", k=k)

    dma("sp", ident[:, :], cst_d[:, C_ID:C_ID + 128], writes=[cst_b])
    dma("sp", trilp[:, :], cst_d[:, C_TRIL:C_TRIL + 128], writes=[cst_b])
    dma("sp", upp[:, :], cst_d[:, C_UPP:C_UPP + 128], writes=[cst_b])
    dma("sp", inv2[:, :], cst_d[:, C_INV:C_INV + 1], writes=[cst_b], slow=True)
    dma("pool", maskA[:, :], cst_d[:, C_MASK:C_MASK + 256], writes=[cst_b])
    dma("pool", rmat[:, :], cst_d[:, C_RMAT:C_RMAT + 128], writes=[cst_b])
    dma("pool", ident_bf[:, :], cst_d[:, C_ID:C_ID + 128], writes=[cst_b])
    dma("pool", mbias[:, :], cst_d[:, C_MBIAS:C_MBIAS + 256], writes=[cst_b])
    dma("sp", vecs[:, :, :], vec_d.rearrange("l p n -> p l n"), writes=[cst_b])
    memset("dve", T(ones_bf[:, :], cst_b), 1.0)
    memset("dve", T(ones32[:, :], cst_b), 1.0)
    memset("dve", T(epsT[:, :], cst_b), EPS)
    P.barrier()
    ONES = T(ones_bf[:, :], cst_b)
    MASKA = T(maskA[:, :], cst_b)
    MASKC = T(maskA[:, 0:128], cst_b)
    RMAT = T(rmat[:, :], cst_b)
    EPSB = T(epsT[:, :], cst_b)

    def xt(c, i, w=512):
        return T(xT[:, c, i * 512:i * 512 + w], xT_b[c][i])

    def rstd_from_ss(ss_ps, n_feat, tmp_ring, parts=128):
        t = tmp_ring.next()
        tp = sub(t, t.ap[0:parts, :])
        act(tp, sub(ss_ps, ss_ps.ap[0:parts, :]), AF.Ln, scale=1.0 / n_feat, bias=sub(EPSB, epsT[0:parts, :]))
        act(tp, tp, AF.Exp, scale=-0.5)
        return tp

    def rmsnorm_h(l, goff, hT, hT_b, tiles, f32ring, sqring):
        for k, i in enumerate(tiles):
            ss = bank()
            for c in range(8):
                sq = sqring.next()
                act(sq, xt(c, i), AF.Square)
                mm(ss, ONES, sq, start=(c == 0), stop=(c == 7))
            rs = rstd_from_ss(ss, D, f32ring)
            for c in range(8):
                stt("dve", T(hT[:, c, k * 512:(k + 1) * 512], hT_b[c][k]), xt(c, i), V(l, goff + c), rs, ALU.mult, ALU.mult)

    for sq_i in range(nseq):
        phase(0)
        P.barrier()
        reg.reset()
        set_pool(range(8))
        xin = Ring([reg.f32(1024) for _ in range(3)])
        for tt_i in range(16):
            xi = xin.next()
            dma("sp", xi.ap, x_d[sq_i, tt_i * 128:(tt_i + 1) * 128, :], writes=xi.bufs)
            for half in range(2):
                pb = bank()
                for cc in range(4):
                    c = half * 4 + cc
                    tr(sub(pb, pb.ap[:, cc * 128:(cc + 1) * 128]), sub(xi, xi.ap[:, c * 128:(c + 1) * 128]))
                dst = T(xT[:, half * 4:half * 4 + 4, tt_i * 128:(tt_i + 1) * 128],
                        [xT_b[half * 4 + cc][tt_i // 4] for cc in range(4)])
                cp("act" if half == 0 else "dve", dst, sub(pb, pb.ap.rearrange("p (c t) -> p c t", c=4)))
        phase(1)
        P.barrier()
        reg.reset()
        tA = T(reg.f32(S), Buf())
        tB = T(reg.f32(S), Buf())
        tC = T(reg.f32(S), Buf())
        for pi_ in range(4):
            dma("sp", posi_t[:, :], pos_d[sq_i, :, pi_ * 512:(pi_ + 1) * 512], writes=[posi_b])
            cp("pool", sub(tA, tA.ap[:, pi_ * 512:(pi_ + 1) * 512]), T(posi_t[:, :], posi_b))
        ts("dve", tA, tA, T(inv2[:, 0:1], cst_b), None, ALU.mult)
        MG = 12582912.0
        TAB = T(sinT[:, :], tab_b)
        TABC = T(cosT[:, :], tab_b)
        for which, shift, dst in ((0, 0.0, TAB), (1, 0.25, TABC)):
            ts("dve", tB, tA, 1.0 / (2 * math.pi), shift, ALU.mult, ALU.add)
            ts("dve", tC, tB, MG, -MG, ALU.add, ALU.add)
            tt("dve", tB, tB, tC, ALU.subtract)
            act(dst, tB, AF.Sin, scale=6.283185)

        for l in range(nlayers):
            phase(2)
            P.barrier()
            reg.reset()
            set_pool(range(8))
            hT = reg.bf(8 * S).rearrange("p (c t) -> p c t", c=8)
            hT_b = [[Buf() for _ in range(NT)] for _ in range(8)]
            mg_off = reg.off
            mg = reg.bf(8 * S).rearrange("p (c t) -> p c t", c=8)
            mg_b = [[Buf() for _ in range(NT)] for _ in range(8)]
            yT_off = reg.off
            yT = reg.bf(4 * S).rearrange("p (c t) -> p c t", c=4)
            yT_b = [[Buf() for _ in range(NT)] for _ in range(4)]
            base_off = reg.off
            f32ring = Ring([reg.f32(512) for _ in range(3)])
            sqring = Ring([reg.bf(512) for _ in range(3)])
            rmsnorm_h(l, V_PRE_MIX, hT, hT_b, list(range(NT)), f32ring, sqring)
            if sq_i == 0 and l == 0:
                tap("hT0", T(hT[:, 0, :], [hT_b[0][i] for i in range(NT)]))

            def hTt(c, i):
                return T(hT[:, c, i * 512:(i + 1) * 512], hT_b[c][i])

            def hT_all(c):
                return [hT_b[c][i] for i in range(NT)]

            def merge_pass(g, first):
                set_pool(range(8))
                for j in range(8):
                    k = wload(l, W_MRG + (g * 8 + j) * 1536, 1536)
                    wg = wblk(k, 0, 8, 128)
                    wb = wblk(k, 1024, 4, 128)
                    for i in range(NT):
                        gp = bank()
                        for c in range(8):
                            mm(gp, sub(wg, wg.ap[:, c, :]), hTt(c, i), start=(c == 0), stop=(c == 7))
                        pp = bank()
                        for c in range(4):
                            mm(pp, sub(wb, wb.ap[:, c, :]), T(yT[:, c, i * 512:(i + 1) * 512], yT_b[c][i]),
                               start=(c == 0), stop=(c == 3))
                        sg = mtmp.next()
                        act(sg, gp, AF.Sigmoid, bias=V(l, V_BGATE + g * 8 + j))
                        mt = T(mg[:, j, i * 512:(i + 1) * 512], mg_b[j][i])
                        if first:
                            tt("dve", mt, sg, pp, ALU.mult)
                        else:
                            sgb = T(sg.ap.bitcast(BF16)[:, 0:512], sg.bufs)
                            tt("dve", sgb, sg, pp, ALU.mult)
                            tt("pool", mt, mt, sgb, ALU.add)

            phase(3)
            P.barrier()
            reg.reset(base_off)
            oacc = reg.f32(S)
            dacc = reg.f32(S)
            oacc_b = Buf()
            dacc_b = Buf()
            pring = Ring([reg.bf(256) for _ in range(4)])
            qsb = Ring([reg.bf(512) for _ in range(2)])
            usb = Ring([reg.bf(512) for _ in range(2)])
            t1r = Ring([reg.f32(512) for _ in range(3)])
            mtmp = t1r
            att_end = reg.off
            reg.reset(mg_off)
            qkv = []
            for _ in range(2):
                qkv.append(dict(q=reg.bf(S), k=reg.bf(S), v=reg.bf(S),
                                qb=[Buf() for _ in range(NT)], kb=[Buf() for _ in range(NT)],
                                vb=[Buf() for _ in range(NT)]))
            vts = reg.bf(S)
            vts_b = [Buf() for _ in range(NT)]
            assert reg.off <= yT_off
            reg.reset(att_end)
            scale = 128.0 ** -0.5
            units = [(h, g) for h in range(4) for g in GSET]
            set_pool([0, 1, 2], "b")
            set_pool([3, 4, 5], "a")
            odbank = [T(ps[6][:, :], ps_b[6]), T(ps[7][:, :], ps_b[7])]

            def proj_gen(u):
                h, g = units[u]
                bset = qkv[u % 2]
                d = DILS[g]
                k = wload(l, W_ATT + (g * 4 + h) * 3072, 3072)
                wq = wblk(k, 0, 8, 128)
                wk = wblk(k, 1024, 8, 128)
                wv = wblk(k, 2048, 8, 128)
                for (w_, dst, dst_b) in ((wq, bset["q"], bset["qb"]), (wk, bset["k"], bset["kb"])):
                    pendR = None
                    for i in range(NT + 1):
                        if i < NT:
                            pp = bank("a")
                            for c in range(8):
                                mm(pp, sub(w_, w_.ap[:, c, :]), hTt(c, i), start=(c == 0), stop=(c == 7))
                            qs = qsb.next()
                            cp("act", qs, pp)
                            t1 = t1r.next()
                            tt("pool", t1, qs, T(cosT[:, i * 512:(i + 1) * 512], tab_b), ALU.mult)
                            us = usb.next()
                            tt("pool", us, qs, T(sinT[:, i * 512:(i + 1) * 512], tab_b), ALU.mult)
                        if pendR is not None:
                            (pus, pt1, pi) = pendR
                            rp = bank("a")
                            mm(rp, RMAT, pus)
                            wb_ = [dst_b[pi]] if d == 1 else list(dst_b)
                            tt("dve", T(permw(dst, d, pi), wb_), sub(pt1, shp2(pt1.ap, d)), sub(rp, shp2(rp.ap, d)), ALU.add)
                        pendR = (us, t1, i) if i < NT else None
                        yield
                for i in range(NT):
                    pp = bank("a")
                    for c in range(8):
                        mm(pp, sub(wv, wv.ap[:, c, :]), hTt(c, i), start=(c == 0), stop=(c == 7))
                    wb_ = [vts_b[i]] if d == 1 else list(vts_b)
                    cp("act", T(permw(vts, d, i), wb_), sub(pp, shp2(pp.ap, d)))
                    yield
                for i in range(NT):
                    pp = bank("a")
                    for bb in range(4):
                        B = i * 4 + bb
                        mm(sub(pp, pp.ap[:, bb * 128:(bb + 1) * 128]), T(vts[:, B * 128:(B + 1) * 128], vts_b[i]),
                           T(ident_bf[:, :], cst_b))
                    cp("act", T(bset["v"][:, i * 512:(i + 1) * 512], bset["vb"][i]), pp)
                    yield

            def core_gen(u):
                h, g = units[u]
                bset = qkv[u % 2]
                qT, kT, vtk = bset["q"], bset["k"], bset["v"]
                qT_b, kT_b, v_b = bset["qb"], bset["kb"], bset["vb"]
                d = DILS[g]
                L = S // d
                nb = L // 128
                first = (g == GSET[0])
                last = (g == GSET[-1])
                pt = {}

                def scores(B):
                    n = B % nb
                    wq_ = 256 if n + 1 < nb else 128
                    sp_ = bank("b")
                    rb = [qT_b[(B * 128) // 512]]
                    if wq_ == 256 and ((B + 1) * 128) // 512 != (B * 128) // 512:
                        rb.append(qT_b[((B + 1) * 128) // 512])
                    mm(sub(sp_, sp_.ap[:, 0:wq_]), T(kT[:, B * 128:(B + 1) * 128], kT_b[B // 4]),
                       T(qT[:, B * 128:B * 128 + wq_], rb), start=True, stop=False)
                    mm(sub(sp_, sp_.ap[:, 0:wq_]), T(ident_bf[:, :], cst_b), T(mbias[:, 0:wq_], cst_b), start=False, stop=True)
                    p_ = pring.next()
                    act(sub(p_, p_.ap[:, 0:wq_]), sub(sp_, sp_.ap[:, 0:wq_]), AF.Exp, scale=scale)
                    pt[B] = p_

                scores(0)
                yield
                for B in range(16):
                    if B + 1 < 16:
                        scores(B + 1)
                    n = B % nb
                    i2 = B // 2
                    ob = odbank[i2 % 2]
                    osl = sub(ob, ob.ap[:, (B % 2) * 128:(B % 2 + 1) * 128])
                    dsl = sub(ob, ob.ap[:, 256 + (B % 2) * 128:256 + (B % 2 + 1) * 128])
                    vcur = T(vtk[:, B * 128:(B + 1) * 128], v_b[B // 4])
                    pc = sub(pt[B], pt[B].ap[:, 0:128])
                    if n > 0:
                        vprev = T(vtk[:, (B - 1) * 128:B * 128], v_b[(B - 1) // 4])
                        ppv = sub(pt[B - 1], pt[B - 1].ap[:, 128:256])
                        mm(osl, vprev, ppv, start=True, stop=False)
                        mm(osl, vcur, pc, start=False, stop=True)
                        mm(dsl, ONES, ppv, start=True, stop=False)
                        mm(dsl, ONES, pc, start=False, stop=True)
                    else:
                        mm(osl, vcur, pc, start=True, stop=True)
                        mm(dsl, ONES, pc, start=True, stop=True)
                    if B % 2 == 1:
                        ov, kk_ = perm(oacc, d, i2 * 256, 256)
                        dv, _ = perm(dacc, d, i2 * 256, 256)
                        OA = T(ov, oacc_b)
                        DA = T(dv, dacc_b)
                        o_src = sub(ob, shp(ob.ap[:, 0:256], kk_))
                        d_src = sub(ob, shp(ob.ap[:, 256:512], kk_))
                        if first:
                            cp("act", OA, o_src)
                            cp("act", DA, d_src)
                        else:
                            tt("dve", OA, OA, o_src, ALU.add)
                            tt("dve", DA, DA, d_src, ALU.add)
                    yield
                if last:
                    DAF = T(dacc, dacc_b)
                    recip(DAF, DAF)
                    tt("dve", T(yT[:, h, :], [yT_b[h][i] for i in range(NT)]), T(oacc, oacc_b), DAF, ALU.mult)
                    yield

            def drive(ga, gb):
                ga = iter(ga) if ga is not None else None
                gb = iter(gb) if gb is not None else None
                while ga is not None or gb is not None:
                    if ga is not None:
                        try:
                            next(ga)
                        except StopIteration:
                            ga = None
                    if gb is not None:
                        try:
                            next(gb)
                        except StopIteration:
                            gb = None

            phase(3)
            drive(proj_gen(0), None)
            for u in range(len(units)):
                drive(proj_gen(u + 1) if u + 1 < len(units) else None, core_gen(u))
            set_pool(range(8), "a")
            if sq_i == 0 and l == 0:
                tap("ydil", T(yT[:, 0, :], [yT_b[0][i] for i in range(NT)]))
            phase(4)
            merge_pass(2, True)

            phase(5)
            P.barrier()
            reg.reset(base_off)
            set_pool(range(8))
            l1 = reg.f32(16 * 256).rearrange("p (n f) -> p n f", n=16)
            l1_b = [Buf() for _ in range(16)]
            gla_off = reg.off
            alow = reg.f32(S)
            alow_b = [Buf() for _ in range(NT)]
            e1r = Ring([reg.f32(512) for _ in range(2)])
            wups = reg.f32(512)
            wups_b = Buf()
            dma("sp", wups[0:32, :], wup_d[l, :, :], writes=[wups_b])
            k = wload(l, W_ALOW, 256)
            wa = wblk(k, 0, 8, 32)
            for i in range(NT):
                pp = bank()
                for c in range(8):
                    mm(sub(pp, pp.ap[0:32, :]), sub(wa, wa.ap[:, c, :]), hTt(c, i), start=(c == 0), stop=(c == 7))
                cp("act", T(alow[0:32, i * 512:(i + 1) * 512], alow_b[i]), sub(pp, pp.ap[0:32, :]))
            WUP = T(wups[0:32, 0:256], wups_b)
            BVEC = T(wups[0:32, 256:512], wups_b)
            O32 = T(ones32[0:32, :], cst_b)
            for n2 in range(8):
                pp = bank()
                for u in range(2):
                    n = n2 * 2 + u
                    o_ = sub(pp, pp.ap[:, u * 256:(u + 1) * 256])
                    mm(o_, T(alow[0:32, n * 128:(n + 1) * 128], alow_b[n // 4]), WUP, start=True, stop=False)
                    mm(o_, O32, BVEC, start=False, stop=True)
                e1 = e1r.next()
                act(e1, pp, AF.Exp, scale=-1.0)
                act(T(l1[:, n2 * 2:n2 * 2 + 2, :], [l1_b[n2 * 2], l1_b[n2 * 2 + 1]]),
                    sub(e1, e1.ap.rearrange("p (n f) -> p n f", n=2)), AF.Ln, bias=1.0)
            if sq_i == 0 and l == 0:
                tap("l1", T(l1[:, 0, :], l1_b[0]))
            P.barrier()
            reg.reset(gla_off)
            phase(6)
            qqr = Ring([reg.bf(512) for _ in range(2)])
            kkr = Ring([reg.bf(512) for _ in range(2)])
            vhr = Ring([reg.bf(512) for _ in range(2)])
            kinr = Ring([reg.bf(256) for _ in range(2)])
            gsr = Ring([reg.f32(512) for _ in range(1)])
            eLr = Ring([reg.f32(512) for _ in range(1)])
            enLr = Ring([reg.f32(512) for _ in range(1)])
            eRr = Ring([reg.f32(256) for _ in range(1)])
            dec = T(reg.f32(16), Buf())
            Sst = T(reg.f32(128), Buf())
            Sbf = T(reg.bf(128), Buf())
            pmr = Ring([reg.bf(128) for _ in range(2)])
            osq = Ring([reg.bf(512) for _ in range(1)])
            y1r = Ring([reg.f32(512) for _ in range(2)])
            mtmp = y1r
            TRIL = T(trilp[:, :], cst_b)
            UPP = T(upp[:, :], cst_b)
            obank = [T(ps[6][:, :], ps_b[6]), T(ps[7][:, :], ps_b[7])]
            set_pool(range(6))
            for h in range(4):
                k = wload(l, W_GLA + h * 3072, 3072)
                wh = wblk(k, 0, 8, 384)
                memset("dve", sub(Sst, Sst.ap[0:64, :]), 0.0)
                for i in range(NT):
                    lt = bank()
                    for u in range(4):
                        n = i * 4 + u
                        mm(sub(lt, lt.ap[0:64, u * 128:(u + 1) * 128]), T(l1[:, n, h * 64:(h + 1) * 64], l1_b[n]), TRIL)
                    eL = eLr.next()
                    enL = enLr.next()
                    act(sub(eL, eL.ap[0:64, :]), sub(lt, lt.ap[0:64, :]), AF.Exp)
                    act(sub(enL, enL.ap[0:64, :]), sub(lt, lt.ap[0:64, :]), AF.Exp, scale=-1.0)
                    cp("pool", sub(dec, dec.ap[0:64, i * 4:i * 4 + 4]),
                       sub(eL, eL.ap[0:64, :].rearrange("p (u t) -> p u t", u=4)[:, :, 127]))
                    pq = bank()
                    for c in range(8):
                        mm(sub(pq, pq.ap[0:64, :]), sub(wh, wh.ap[:, c, 0:64]), hTt(c, i), start=(c == 0), stop=(c == 7))
                    qq = qqr.next()
                    stt("dve", sub(qq, qq.ap[0:64, :]), sub(pq, pq.ap[0:64, :]), 0.125,
                        sub(eL, eL.ap[0:64, :]), ALU.mult, ALU.mult)
                    pk = bank()
                    for c in range(8):
                        mm(sub(pk, pk.ap[0:64, :]), sub(wh, wh.ap[:, c, 64:128]), hTt(c, i), start=(c == 0), stop=(c == 7))
                    kk = kkr.next()
                    tt("dve", sub(kk, kk.ap[0:64, :]), sub(pk, pk.ap[0:64, :]), sub(enL, enL.ap[0:64, :]), ALU.mult)
                    pg = bank()
                    for c in range(8):
                        mm(pg, sub(wh, wh.ap[:, c, 256:384]), hTt(c, i), start=(c == 0), stop=(c == 7))
                    gs = gsr.next()
                    act(gs, pg, AF.Silu)
                    lr = bank()
                    pkt = bank()
                    for u in range(4):
                        n = i * 4 + u
                        mm(sub(lr, lr.ap[:, u * 64:(u + 1) * 64]), UPP, T(l1[:, n, h * 64:(h + 1) * 64], l1_b[n]))
                        for c in range(8):
                            hv = T(hT[:, c, n * 128:(n + 1) * 128], hT_b[c][i])
                            mm(sub(pkt, pkt.ap[:, u * 64:(u + 1) * 64]), hv, sub(wh, wh.ap[:, c, 64:128]),
                               start=(c == 0), stop=(c == 7))
                    eR = eRr.next()
                    act(eR, sub(lr, lr.ap[:, 0:256]), AF.Exp)
                    kin = kinr.next()
                    tt("dve", kin, sub(pkt, pkt.ap[:, 0:256]), eR, ALU.mult)
                    pvt = bank()
                    for u in range(4):
                        n = i * 4 + u
                        for c in range(8):
                            hv = T(hT[:, c, n * 128:(n + 1) * 128], hT_b[c][i])
                            mm(sub(pvt, pvt.ap[:, u * 128:(u + 1) * 128]), hv, sub(wh, wh.ap[:, c, 128:256]),
                               start=(c == 0), stop=(c == 7))
                    vh = vhr.next()
                    cp("act", vh, pvt)
                    ob = obank[i % 2]
                    for u in range(4):
                        n = i * 4 + u
                        qqc = sub(qq, qq.ap[0:64, u * 128:(u + 1) * 128])
                        kkc = sub(kk, kk.ap[0:64, u * 128:(u + 1) * 128])
                        vc = sub(vh, vh.ap[:, u * 128:(u + 1) * 128])
                        sp_ = bank()
                        mm(sub(sp_, sp_.ap[:, 0:128]), kkc, qqc)
                        pm = pmr.next()
                        stt("dve", pm, sub(sp_, sp_.ap[:, 0:128]), -16.0, TRIL, ALU.mult, ALU.mult)
                        osl = sub(ob, ob.ap[:, u * 128:(u + 1) * 128])
                        if n == 0:
                            mm(osl, vc, pm, start=True, stop=True)
                        else:
                            mm(osl, vc, pm, start=True, stop=False)
                            mm(osl, sub(Sbf, Sbf.ap[0:64, :]), qqc, start=False, stop=True)
                        if n < 15:
                            up_ = bank()
                            mm(sub(up_, up_.ap[0:64, 0:128]), sub(kin, kin.ap[:, u * 64:(u + 1) * 64]), vc)
                            if n == 0:
                                cp("dve", sub(Sst, Sst.ap[0:64, :]), sub(up_, up_.ap[0:64, 0:128]))
                            else:
                                stt("dve", sub(Sst, Sst.ap[0:64, :]), sub(Sst, Sst.ap[0:64, :]),
                                    sub(dec, dec.ap[0:64, n:n + 1]), sub(up_, up_.ap[0:64, 0:128]), ALU.mult, ALU.add)
                            cp("act", sub(Sbf, Sbf.ap[0:64, :]), sub(Sst, Sst.ap[0:64, :]))
                    o2 = osq.next()
                    act(o2, ob, AF.Square)
                    ss = bank()
                    mm(ss, ONES, o2)
                    rs = rstd_from_ss(ss, 128, y1r)
                    y1 = y1r.next()
                    stt("dve", y1, ob, V(l, V_GLAG), rs, ALU.mult, ALU.mult)
                    tt("dve", T(yT[:, h, i * 512:(i + 1) * 512], yT_b[h][i]), y1, gs, ALU.mult)
            if sq_i == 0 and l == 0:
                tap("ygla", T(yT[:, 0, :], [yT_b[0][i] for i in range(NT)]))
            phase(7)
            merge_pass(0, False)

            phase(8)
            P.barrier()
            reg.reset(base_off)
            set_pool(range(8))
            ub = reg.f32(S + 2)
            ub_b = Buf()
            xsr = Ring([reg.f32(512) for _ in range(2)])
            accr = Ring([reg.f32(512) for _ in range(2)])
            mtmp = Ring([reg.f32(512) for _ in range(2)])
            memset("dve", T(ub[:, 0:2], ub_b), 0.0)
            wmx_off = reg.off
            wmx = reg.bf(8192).rearrange("p (a b w) -> p a b w", a=8, b=8)
            wmx_b = Buf()
            for a in range(4):
                dma("pool", wmx[:, 2 * a:2 * a + 2, :, :].rearrange("p a b w -> p (a b w)"),
                    wpk_d[l, :, W_MIX + a * 2048:W_MIX + (a + 1) * 2048], writes=[wmx_b])
            for ch in range(4):
                k = wload(l, W_SC + ch * 3072, 3072)
                wb_ = wblk(k, 0, 8, 128)
                wc_ = wblk(k, 1024, 8, 128)
                wx_ = wblk(k, 2048, 8, 128)
                for i in range(NT):
                    pb_ = bank()
                    pc_ = bank()
                    px_ = bank()
                    for (w_, p_) in ((wb_, pb_), (wc_, pc_), (wx_, px_)):
                        for c in range(8):
                            mm(p_, sub(w_, w_.ap[:, c, :]), hTt(c, i), start=(c == 0), stop=(c == 7))
                    xs = xsr.next()
                    cp("act", xs, px_)
                    UB = T(ub[:, 2 + i * 512:2 + (i + 1) * 512], ub_b)
                    tt("dve", UB, pc_, xs, ALU.mult)
                    acc = accr.next()
                    act(acc, UB, AF.Copy, scale=vecs[:, l, V_SCCW + 2 * 4 + ch:V_SCCW + 2 * 4 + ch + 1])
                    stt("dve", acc, T(ub[:, 1 + i * 512:1 + (i + 1) * 512], ub_b), V(l, V_SCCW + 1 * 4 + ch), acc, ALU.mult, ALU.add)
                    stt("dve", acc, T(ub[:, i * 512:(i + 1) * 512], ub_b), V(l, V_SCCW + 0 * 4 + ch), acc, ALU.mult, ALU.add)
                    tt("dve", T(yT[:, ch, i * 512:(i + 1) * 512], yT_b[ch][i]), pb_, acc, ALU.mult)
            if sq_i == 0 and l == 0:
                tap("ysc", T(yT[:, 0, :], [yT_b[0][i] for i in range(NT)]))
            merge_pass(1, False)
            if sq_i == 0 and l == 0:
                tap("mg", T(mg[:, 0, :], [mg_b[0][i] for i in range(NT)]))

            phase(9)
            P.barrier()
            reg.reset(yT_off)
            set_pool(range(8))
            mo2 = [R[:, kk2 * 8192:(kk2 + 1) * 8192].bitcast(F32).rearrange("p (c t) -> p c t", c=8) for kk2 in range(2)]
            mo2_b = [[Buf() for _ in range(8)] for _ in range(2)]
            sqr = Ring([reg.bf(512) for _ in range(3)])
            rsr = Ring([reg.f32(512) for _ in range(2)])
            tmr = Ring([reg.f32(512) for _ in range(3)])
            assert reg.off <= wmx_off, (reg.off, wmx_off)
            set_pool(range(6))
            ssm = [T(ps[6][:, :], ps_b[6]), T(ps[7][:, :], ps_b[7])]
            for i in range(NT):
                ss = ssm[i % 2]
                mo = mo2[i % 2]
                mo_b = mo2_b[i % 2]
                pendS = None
                for jo in range(8):
                    pp = bank()
                    for j in range(8):
                        mm(pp, T(wmx[:, jo, j, :], wmx_b), T(mg[:, j, i * 512:(i + 1) * 512], mg_b[j][i]),
                           start=(j == 0), stop=(j == 7))
                    cp("dve", T(mo[:, jo, :], mo_b[jo]), pp)
                    s2 = sqr.next()
                    act(s2, T(mo[:, jo, :], mo_b[jo]), AF.Square)
                    if pendS is not None:
                        mm(ss, ONES, pendS[0], start=(pendS[1] == 0), stop=False)
                    pendS = (s2, jo)
                mm(ss, ONES, pendS[0], start=False, stop=True)
                rs = rstd_from_ss(ss, D, rsr)
                for jo in range(8):
                    tm = tmr.next()
                    tt("pool", tm, T(mo[:, jo, :], mo_b[jo]), rs, ALU.mult)
                    stt("dve", xt(jo, i), tm, V(l, V_POST_MIX + jo), xt(jo, i), ALU.mult, ALU.add)
            if sq_i == 0 and l == 0:
                tap("xmid", T(xT[:, 0, :], [xT_b[0][i] for i in range(NT)]))

            phase(10)
            P.barrier()
            reg.reset()
            hal = reg.f32(NFF * 2).rearrange("p (f t) -> p f t", f=NFF)
            hal_b = Buf()
            ffbase = reg.off
            for half in range(2):
                P.barrier()
                reg.reset(ffbase)
                set_pool(range(8))
                hTh = reg.bf(8 * 1024).rearrange("p (c t) -> p c t", c=8)
                hTh_b = [[Buf() for _ in range(2)] for _ in range(8)]
                hid = reg.bf(NFF * 1024).rearrange("p (f t) -> p f t", f=NFF)
                hid_b = [[Buf() for _ in range(2)] for _ in range(NFF)]
                yo = reg.f32(8 * 1024).rearrange("p (c t) -> p c t", c=8)
                yo_b = [[Buf() for _ in range(2)] for _ in range(8)]
                gbr = [(reg.f32(1026), Buf()) for _ in range(2)]
                accr = Ring([reg.f32(512) for _ in range(2)])
                ger = Ring([reg.f32(512) for _ in range(2)])
                f32ring = Ring([reg.f32(512) for _ in range(2)])
                sqring = Ring([reg.bf(512) for _ in range(3)])
                rmsnorm_h(l, V_PRE_FFN, hTh, hTh_b, [half * 2, half * 2 + 1], f32ring, sqring)
                for f in range(NFF):
                    k = wload(l, W_FF + f * 2048, 2048)
                    wg = wblk(k, 0, 8, 128)
                    wu = wblk(k, 1024, 8, 128)
                    gb, gb_b = gbr[f % 2]
                    if half == 0:
                        memset("dve", T(gb[:, 0:2], gb_b), 0.0)
                    else:
                        cp("dve", T(gb[:, 0:2], gb_b), T(hal[:, f, :], hal_b))
                    for i in range(2):
                        pg = bank()
                        pu = bank()
                        for (w_, p_) in ((wg, pg), (wu, pu)):
                            for c in range(8):
                                mm(p_, sub(w_, w_.ap[:, c, :]), T(hTh[:, c, i * 512:(i + 1) * 512], hTh_b[c][i]),
                                   start=(c == 0), stop=(c == 7))
                        GB = T(gb[:, 2 + i * 512:2 + (i + 1) * 512], gb_b)
                        cp("act", GB, pg)
                        acc = accr.next()
                        act(acc, pg, AF.Identity, scale=vecs[:, l, V_FFCW + 2 * NFF + f:V_FFCW + 2 * NFF + f + 1],
                            bias=V(l, V_FFCB + f))
                        stt("dve", acc, T(gb[:, 1 + i * 512:1 + (i + 1) * 512], gb_b), V(l, V_FFCW + 1 * NFF + f), acc, ALU.mult, ALU.add)
                        stt("dve", acc, T(gb[:, i * 512:(i + 1) * 512], gb_b), V(l, V_FFCW + 0 * NFF + f), acc, ALU.mult, ALU.add)
                        ge = ger.next()
                        act(ge, acc, AF.Gelu_apprx_tanh)
                        tt("dve", T(hid[:, f, i * 512:(i + 1) * 512], hid_b[f][i]), ge, pu, ALU.mult)
                    if half == 0:
                        cp("dve", T(hal[:, f, :], hal_b), T(gb[:, 1024:1026], gb_b))
                ssb = [T(ps[6][:, :], ps_b[6]), T(ps[7][:, :], ps_b[7])]
                set_pool(range(6))
                pendS = None
                for j in range(8):
                    k = wload(l, W_DN + j * 2816, 2816)
                    wd = wblk(k, 0, NFF, 128)
                    for i in range(2):
                        pp = bank()
                        for f in range(NFF):
                            mm(pp, sub(wd, wd.ap[:, f, :]), T(hid[:, f, i * 512:(i + 1) * 512], hid_b[f][i]),
                               start=(f == 0), stop=(f == NFF - 1))
                        cp("dve", T(yo[:, j, i * 512:(i + 1) * 512], yo_b[j][i]), pp)
                        s2 = sqring.next()
                        act(s2, T(yo[:, j, i * 512:(i + 1) * 512], yo_b[j][i]), AF.Square)
                        if pendS is not None:
                            mm(ssb[pendS[2]], ONES, pendS[0], start=(pendS[1] == 0), stop=(pendS[1] == 7))
                        pendS = (s2, j, i)
                mm(ssb[pendS[2]], ONES, pendS[0], start=(pendS[1] == 0), stop=(pendS[1] == 7))
                for i in range(2):
                    rs = rstd_from_ss(ssb[i], D, f32ring)
                    gi = half * 2 + i
                    for j in range(8):
                        tm = ger.next()
                        tt("pool", tm, T(yo[:, j, i * 512:(i + 1) * 512], yo_b[j][i]), rs, ALU.mult)
                        stt("dve", xt(j, gi), tm, V(l, V_POST_FFN + j), xt(j, gi), ALU.mult, ALU.add)
            if sq_i == 0 and l == 0:
                tap("xl0", T(xT[:, 0, :], [xT_b[0][i] for i in range(NT)]))

        phase(0)
        P.barrier()
        reg.reset()
        set_pool(range(8))
        oring = Ring([reg.f32(1024) for _ in range(3)])
        for tt_i in range(16):
            ot = oring.next()
            for half in range(2):
                pb = bank()
                for cc in range(4):
                    c = half * 4 + cc
                    tr(sub(pb, pb.ap[:, cc * 128:(cc + 1) * 128]),
                       T(xT[:, c, tt_i * 128:(tt_i + 1) * 128], xT_b[c][tt_i // 4]))
                cp("act" if half == 0 else "dve", sub(ot, ot.ap[:, half * 512:(half + 1) * 512]), pb)
            dma("sp", out_d[sq_i, tt_i * 128:(tt_i + 1) * 128, :], ot.ap, reads=ot.bufs)

    wflush()
    fin = P.add("sp", None)
    fin.deps = [(op, "raw") for op in P.dma_last.values()]
    stats = P.emit(nc, st)
    print("ops/waits per engine", stats)
    st.close()
    return nc


def _blk(Wc):
    K, w = Wc.shape
    return np.ascontiguousarray(Wc.reshape(K // 128, 128, w).transpose(1, 0, 2)).reshape(128, -1)


def pack_weights(w_in, w_gate, w_branch, w_mix_out, w_ff_gate, w_ff_up, w_ff_down):
    Ls = w_in.shape[0]
    out = np.zeros((Ls, 128, W_TOT), np.float32)
    for l in range(Ls):
        wi = w_in[l]
        o = out[l]
        for g in range(3):
            for h in range(4):
                b = W_ATT + (g * 4 + h) * 3072
                cq = (g * 4 + h) * 128
                o[:, b:b + 1024] = _blk(wi[:, O_DQ + cq:O_DQ + cq + 128])
                o[:, b + 1024:b + 2048] = _blk(wi[:, O_DK + cq:O_DK + cq + 128])
                o[:, b + 2048:b + 3072] = _blk(wi[:, O_DV + cq:O_DV + cq + 128])
        for h in range(4):
            cat = np.concatenate([wi[:, O_GQ + h * 64:O_GQ + (h + 1) * 64], wi[:, O_GK + h * 64:O_GK + (h + 1) * 64],
                                  wi[:, O_GV + h * 128:O_GV + (h + 1) * 128], wi[:, O_GO + h * 128:O_GO + (h + 1) * 128]], axis=1)
            o[:, W_GLA + h * 3072:W_GLA + (h + 1) * 3072] = _blk(cat)
        al = np.zeros((D, 32), np.float32)
        al[:, 0:16] = wi[:, O_AL:O_AL + 16]
        o[:, W_ALOW:W_ALOW + 256] = _blk(al)
        for ch in range(4):
            b = W_SC + ch * 3072
            o[:, b:b + 1024] = _blk(wi[:, O_SB + ch * 128:O_SB + (ch + 1) * 128])
            o[:, b + 1024:b + 2048] = _blk(wi[:, O_SC + ch * 128:O_SC + (ch + 1) * 128])
            o[:, b + 2048:b + 3072] = _blk(wi[:, O_SX + ch * 128:O_SX + (ch + 1) * 128])
        for g in range(3):
            for j in range(8):
                b = W_MRG + (g * 8 + j) * 1536
                o[:, b:b + 1024] = _blk(w_gate[l][:, g * D + j * 128:g * D + (j + 1) * 128])
                o[:, b + 1024:b + 1536] = _blk(w_branch[l][g][:, j * 128:(j + 1) * 128])
        for jo in range(8):
            o[:, W_MIX + jo * 1024:W_MIX + (jo + 1) * 1024] = _blk(w_mix_out[l][:, jo * 128:(jo + 1) * 128])
        for f in range(NFF):
            b = W_FF + f * 2048
            o[:, b:b + 1024] = _blk(w_ff_gate[l][:, f * 128:(f + 1) * 128])
            o[:, b + 1024:b + 2048] = _blk(w_ff_up[l][:, f * 128:(f + 1) * 128])
        for j in range(8):
            o[:, W_DN + j * 2816:W_DN + (j + 1) * 2816] = _blk(w_ff_down[l][:, j * 128:(j + 1) * 128])
    return out


def pack_vecs(b_gate, pre_mix_g, post_mix_g, pre_ffn_g, post_ffn_g, ff_conv_w, ff_conv_b, sc_conv_w, gla_norm_g):
    Ls = b_gate.shape[0]
    v = np.zeros((Ls, 128, NV), np.float32)

    def col(a):
        return a.reshape(-1, 128).T

    for l in range(Ls):
        v[l, :, V_PRE_MIX:V_PRE_MIX + 8] = col(pre_mix_g[l])
        v[l, :, V_POST_MIX:V_POST_MIX + 8] = col(post_mix_g[l])
        v[l, :, V_PRE_FFN:V_PRE_FFN + 8] = col(pre_ffn_g[l])
        v[l, :, V_POST_FFN:V_POST_FFN + 8] = col(post_ffn_g[l])
        v[l, :, V_BGATE:V_BGATE + 24] = col(b_gate[l])
        for k in range(3):
            v[l, :, V_FFCW + k * NFF:V_FFCW + (k + 1) * NFF] = col(ff_conv_w[l, k])
            v[l, :, V_SCCW + k * 4:V_SCCW + (k + 1) * 4] = col(sc_conv_w[l, k])
        v[l, :, V_FFCB:V_FFCB + NFF] = col(ff_conv_b[l])
        v[l, :, V_GLAG] = gla_norm_g[l]
    return v


def make_consts():
    c = np.zeros((128, NCONST), np.float32)
    c[:, C_ID:C_ID + 128] = np.eye(128, dtype=np.float32)
    j = np.arange(128)[:, None]
    i = np.arange(128)[None, :]
    c[:, C_TRIL:C_TRIL + 128] = np.where(j <= i, -1.0 / 16.0, 0.0)
    c[:, C_UPP:C_UPP + 128] = np.where(j > i, -1.0 / 16.0, 0.0)
    c[:, C_MASK:C_MASK + 128] = (j <= i)
    c[:, C_MASK + 128:C_MASK + 256] = (j >= i)
    r = np.zeros((128, 128), np.float32)
    for m in range(64):
        r[m + 64, m] = -1.0
        r[m, m + 64] = 1.0
    c[:, C_RMAT:C_RMAT + 128] = r
    inv = (np.float32(10000.0) ** (-(np.arange(0, 128, 2, dtype=np.float32) / np.float32(128.0)))).astype(np.float32)
    c[:, C_INV] = np.concatenate([inv, inv])
    c[:, C_MBIAS:C_MBIAS + 256] = np.where(c[:, C_MASK:C_MASK + 256] > 0.5, 0.0, -30000.0)
    return c


_NC_CACHE = {}


def prepare_shared(w_in, w_alpha_up, b_alpha, gla_norm_g, sc_conv_w, w_gate, b_gate, w_branch, w_mix_out,
                   pre_mix_g, post_mix_g, pre_ffn_g, post_ffn_g, w_ff_gate, w_ff_up, ff_conv_w, ff_conv_b, w_ff_down):
    f = lambda a: np.asarray(a, np.float32)
    wpk = pack_weights(f(w_in), f(w_gate), f(w_branch), f(w_mix_out), f(w_ff_gate), f(w_ff_up), f(w_ff_down))
    vecs = pack_vecs(f(b_gate), f(pre_mix_g), f(post_mix_g), f(pre_ffn_g), f(post_ffn_g), f(ff_conv_w),
                     f(ff_conv_b), f(sc_conv_w), f(gla_norm_g))
    Ls = wpk.shape[0]
    wup = np.zeros((Ls, 32, 512), np.float32)
    wup[:, 0:16, 0:256] = f(w_alpha_up)
    wup[:, 0, 256:512] = f(b_alpha)
    return dict(wpk=wpk, vecs=vecs, wup=wup, consts=make_consts())


def kernel(x, positions, **weights):
    x = np.asarray(x, np.float32)
    positions = np.asarray(positions, np.int32)
    shared = prepare_shared(**weights)
    n_cores = 8
    per = x.shape[0] // n_cores
    if "nc" not in _NC_CACHE:
        _NC_CACHE["nc"] = build(nseq=per, nlayers=2)
    nc = _NC_CACHE["nc"]
    in_maps = []
    for c in range(n_cores):
        xs = np.ascontiguousarray(x[c * per:(c + 1) * per])
        pb = np.ascontiguousarray(np.broadcast_to(positions[c * per:(c + 1) * per, None, :], (per, 128, S)))
        m = dict(x=xs, posb=pb)
        m.update(shared)
        in_maps.append(m)
    res = run_bass_kernel_spmd(nc, in_maps, core_ids=list(range(n_cores)))
    return np.concatenate([r["out"] for r in res.results], axis=0)
```

```python
import math
from contextlib import ExitStack
import numpy as np
import concourse.bass as bass
import concourse.mybir as mybir
from concourse.bass_utils import run_bass_kernel_spmd

F32 = mybir.dt.float32
BF16 = mybir.dt.bfloat16
I32 = mybir.dt.int32
AF = mybir.ActivationFunctionType
ALU = mybir.AluOpType

PHASE_LOG = []
ENGS = ["pe", "act", "dve", "pool", "sp"]
N_DMA_SEMS = 8

S = 2048
D = 1024
NT = 4
DFF = 2816
NFF = 22
EPS = 1e-6
DILS = (1, 4, 16)
import os
GSET = [int(c) for c in os.environ.get('ATT_G', '012')]
ATT_LVL = int(os.environ.get('ATT_LVL', '9'))
ROPE_ADD_ENG = os.environ.get('ROPE_ADD_ENG', 'pool')
ATT_SUB = int(os.environ.get('ATT_SUB', '9'))
RES_ENG = os.environ.get('RES_ENG', 'dve')
MIX_SUB = int(os.environ.get('MIX_SUB', '9'))
O_GQ, O_GK, O_GV, O_GO, O_AL, O_SB, O_SC, O_SX, O_DQ, O_DK, O_DV = (
    0, 256, 512, 1024, 1536, 1552, 2064, 2576, 3088, 4624, 6160)

W_ATT = 0
W_GLA = W_ATT + 12 * 3072
W_ALOW = W_GLA + 4 * 3072
W_SC = W_ALOW + 256
W_MRG = W_SC + 4 * 3072
W_MIX = W_MRG + 24 * 1536
W_FF = W_MIX + 8192
W_DN = W_FF + NFF * 2048
W_TOT = W_DN + 8 * 2816

V_PRE_MIX, V_POST_MIX, V_PRE_FFN, V_POST_FFN = 0, 8, 16, 24
V_BGATE = 32
V_FFCW = 56
V_FFCB = 122
V_SCCW = 144
V_GLAG = 156
NV = 160

C_ID, C_TRIL, C_UPP, C_MASK, C_RMAT, C_INV, C_MBIAS = 0, 128, 256, 384, 640, 768, 772
NCONST = 772 + 256


class Buf:
    __slots__ = ("w", "rs", "rd")

    def __init__(self):
        self.w = None
        self.rs = {}
        self.rd = []


class Op:
    __slots__ = ("eng", "idx", "fn", "deps", "waited", "seq", "dma", "dsem", "dval", "needed")


class Prog:
    def __init__(self):
        self.ops = {e: [] for e in ENGS}
        self.dma_cnt = {}
        self.dma_last = {}
        self.dma_rr = {e: 0 for e in ENGS}
        self.last_real = {}

    def add(self, eng, fn, reads=(), writes=(), dma=False):
        op = Op()
        op.eng = eng
        op.idx = len(self.ops[eng])
        op.fn = fn
        op.dma = dma
        op.waited = False
        op.seq = 0
        op.needed = []
        deps = []
        for b in reads:
            if b.w is not None:
                deps.append((b.w, "raw"))
        for b in writes:
            if b.w is not None:
                deps.append((b.w, "waw"))
            for r in b.rs.values():
                deps.append((r, "war"))
            for r in b.rd:
                deps.append((r, "war"))
        if dma:
            k = self.dma_rr[eng] % N_DMA_SEMS
            self.dma_rr[eng] += 1
            key = (eng, k)
            prev = self.dma_last.get(key)
            if prev is not None:
                deps.append((prev, "sem"))
            self.dma_cnt[key] = self.dma_cnt.get(key, 0) + 1
            op.dsem = key
            op.dval = 16 * self.dma_cnt[key]
            self.dma_last[key] = op
        for b in reads:
            if dma:
                b.rd.append(op)
            else:
                b.rs[eng] = op
        for b in writes:
            b.w = op
            b.rs = {}
            b.rd = []
        op.deps = deps
        self.ops[eng].append(op)
        if fn is not None and not dma:
            self.last_real[eng] = op
        return op

    def barrier(self):
        lasts = list(self.last_real.values()) + list(self.dma_last.values())
        for e in ENGS:
            op = self.add(e, None)
            op.deps = [(p, "raw") for p in lasts]

    def finalize(self):
        for eng in ENGS:
            known = {}
            for op in self.ops[eng]:
                need = {}
                for (p, kind) in op.deps:
                    if p.dma:
                        key = ("dma",) + p.dsem
                        val = p.dval
                    else:
                        if p.eng == eng and eng == "pe":
                            continue
                        key = p.eng
                        val = p.idx + 1
                    if known.get(key, 0) >= val:
                        continue
                    if key not in need or need[key][0] < val:
                        need[key] = (val, p)
                for key, (val, p) in need.items():
                    known[key] = val
                    p.waited = True
                    op.needed.append(p)
        for eng in ENGS:
            n = 0
            for op in self.ops[eng]:
                if (not op.dma) and op.waited:
                    n += 1
                    op.seq = n

    def emit(self, nc, stack):
        self.finalize()
        esem = {e: stack.enter_context(nc.semaphore("s_" + e)) for e in ENGS}
        dsem = {}
        for key in self.dma_cnt:
            dsem[key] = stack.enter_context(nc.semaphore("d_%s%d" % key))
        block = stack.enter_context(nc.Block())

        def run(eng_name):
            def body(e):
                for op in self.ops[eng_name]:
                    for p in op.needed:
                        if p.dma:
                            e.wait_ge(dsem[p.dsem], p.dval)
                        else:
                            e.wait_ge(esem[p.eng], p.seq)
                    if op.fn is None:
                        continue
                    ins = op.fn(e)
                    if op.dma:
                        ins.then_inc(dsem[op.dsem], 16)
                    elif op.waited:
                        ins.then_inc(esem[eng_name], 1)
            return body

        block.tensor(run("pe"))
        block.scalar(run("act"))
        block.vector(run("dve"))
        block.gpsimd(run("pool"))
        block.sync(run("sp"))
        return {e: (len(self.ops[e]), sum(len(o.needed) for o in self.ops[e])) for e in ENGS}


class T:
    __slots__ = ("ap", "bufs")

    def __init__(self, ap, bufs):
        self.ap = ap
        self.bufs = list(bufs) if isinstance(bufs, (list, tuple)) else [bufs]


def build(nseq=2, nlayers=2, taps=None, stop=99):
    taps = taps or {}
    nc = bass.Bass("TRN2", target_bir_lowering=False)
    x_d = nc.dram_tensor("x", [nseq, S, D], F32, kind="ExternalInput").ap()
    pos_d = nc.dram_tensor("posb", [nseq, 128, S], I32, kind="ExternalInput").ap()
    wpk_d = nc.dram_tensor("wpk", [2, 128, W_TOT], F32, kind="ExternalInput").ap()
    vec_d = nc.dram_tensor("vecs", [2, 128, NV], F32, kind="ExternalInput").ap()
    wup_d = nc.dram_tensor("wup", [2, 32, 512], F32, kind="ExternalInput").ap()
    cst_d = nc.dram_tensor("consts", [128, NCONST], F32, kind="ExternalInput").ap()
    out_d = nc.dram_tensor("out", [nseq, S, D], F32, kind="ExternalOutput").ap()
    tap_d = {}
    for name, shape in taps.items():
        tap_d[name] = nc.dram_tensor("tap_" + name, list(shape), F32, kind="ExternalOutput").ap()

    P = Prog()
    st = ExitStack()
    skip = [False]
    _orig_add = P.add

    def _add(eng, fn, reads=(), writes=(), dma=False):
        if skip[0]:
            return Op()
        return _orig_add(eng, fn, reads, writes, dma)
    P.add = _add

    def phase(k):
        skip[0] = (k > stop)
        PHASE_LOG.append((k, len(P.ops["pe"])))

    def sb(name, shape, dt):
        return st.enter_context(nc.sbuf_tensor(name, list(shape), dt))

    xT = sb("xT", [128, 8, S], F32)
    xT_b = [[Buf() for _ in range(NT)] for _ in range(8)]
    cosT = sb("cosT", [128, S], BF16)
    sinT = sb("sinT", [128, S], BF16)
    tab_b = Buf()
    ident = sb("ident", [128, 128], F32)
    trilp = sb("trilp", [128, 128], F32)
    upp = sb("upp", [128, 128], F32)
    inv2 = sb("inv2", [128, 1], F32)
    maskA = sb("maskA", [128, 256], BF16)
    rmat = sb("rmat", [128, 128], BF16)
    ident_bf = sb("ident_bf", [128, 128], BF16)
    mbias = sb("mbias", [128, 256], BF16)
    ones_bf = sb("ones_bf", [128, 128], BF16)
    ones32 = sb("ones32", [32, 128], F32)
    epsT = sb("epsT", [128, 1], F32)
    vecs = sb("vecs_sb", [128, 2, NV], F32)
    cst_b = Buf()
    wst = [sb("wst%d" % i, [128, 3072], BF16) for i in range(2)]
    wst_b = [Buf(), Buf()]
    wst_rr = [0]
    RN = 59648
    posi_t = sb("posi", [128, 512], I32)
    posi_b = Buf()
    R = sb("R", [128, RN], BF16)
    print("sbuf bytes remaining", nc.sbuf_bytes_remaining)

    ps = [st.enter_context(nc.psum_tensor("ps%d" % i, [128, 512], F32)) for i in range(8)]
    ps_b = [Buf() for _ in range(8)]
    pool_banks = {"a": list(range(8)), "b": []}
    ps_rr = {"a": 0, "b": 0}

    def set_pool(banks, which="a"):
        pool_banks[which] = list(banks)
        ps_rr[which] = 0

    def bank(which="a"):
        k = pool_banks[which][ps_rr[which] % len(pool_banks[which])]
        ps_rr[which] += 1
        return T(ps[k][:, :], ps_b[k])

    def bufs_of(*ts):
        out = []
        for t in ts:
            if t is None or not isinstance(t, T):
                continue
            out.extend(t.bufs)
        return out

    def apof(t):
        return t.ap if isinstance(t, T) else t

    def sub(t, ap):
        return T(ap, t.bufs)

    def mm(out, lhsT, rhs, start=True, stop=True):
        o, l, r = out.ap, lhsT.ap, rhs.ap
        P.add("pe", lambda e: e.matmul(o, lhsT=l, rhs=r, start=start, stop=stop),
              reads=bufs_of(lhsT, rhs), writes=bufs_of(out))

    def tr(out, in_):
        o, i = out.ap, in_.ap
        P.add("pe", lambda e: e.transpose(o, i, ident[:, :]), reads=bufs_of(in_) + [cst_b], writes=bufs_of(out))

    def act(out, in_, func, scale=1.0, bias=None, extra_reads=()):
        o, i = out.ap, in_.ap
        kw = {}
        if bias is not None:
            kw["bias"] = apof(bias)
        P.add("act", lambda e: e.activation(out=o, in_=i, func=func, scale=scale, **kw),
              reads=bufs_of(in_, bias) + list(extra_reads), writes=bufs_of(out))

    def cp(eng, out, in_):
        o, i = out.ap, in_.ap
        if eng == "act":
            P.add("act", lambda e: e.copy(out=o, in_=i), reads=bufs_of(in_), writes=bufs_of(out))
        else:
            P.add(eng, lambda e: e.tensor_copy(out=o, in_=i), reads=bufs_of(in_), writes=bufs_of(out))

    def tt(eng, out, in0, in1, op):
        o, a, b = out.ap, in0.ap, in1.ap
        P.add(eng, lambda e: e.tensor_tensor(out=o, in0=a, in1=b, op=op),
              reads=bufs_of(in0, in1), writes=bufs_of(out))

    def ts(eng, out, in0, s1, s2, op0, op1=None):
        o, a = out.ap, in0.ap
        a1, a2 = apof(s1), apof(s2)
        if op1 is None:
            P.add(eng, lambda e: e.tensor_scalar(out=o, in0=a, scalar1=a1, scalar2=None, op0=op0),
                  reads=bufs_of(in0, s1), writes=bufs_of(out))
        else:
            P.add(eng, lambda e: e.tensor_scalar(out=o, in0=a, scalar1=a1, scalar2=a2, op0=op0, op1=op1),
                  reads=bufs_of(in0, s1, s2), writes=bufs_of(out))

    def stt(eng, out, in0, scalar, in1, op0, op1):
        o, a, b = out.ap, in0.ap, in1.ap
        sc = apof(scalar)
        P.add(eng, lambda e: e.scalar_tensor_tensor(out=o, in0=a, scalar=sc, in1=b, op0=op0, op1=op1),
              reads=bufs_of(in0, scalar, in1), writes=bufs_of(out))

    def memset(eng, out, val):
        o = out.ap
        P.add(eng, lambda e: e.memset(o, val), writes=bufs_of(out))

    def recip(out, in_):
        o, i = out.ap, in_.ap
        P.add("dve", lambda e: e.reciprocal(out=o, in_=i), reads=bufs_of(in_), writes=bufs_of(out))

    def dma(eng, out, in_, reads=(), writes=(), slow=False):
        if slow:
            P.add(eng, lambda e: e.dma_start(out=out, in_=in_, allow_slow_non_contiguous=True),
                  reads=list(reads), writes=list(writes), dma=True)
        else:
            P.add(eng, lambda e: e.dma_start(out=out, in_=in_), reads=list(reads), writes=list(writes), dma=True)

    def tap(name, src):
        if name in tap_d:
            d = tap_d[name]
            dma("pool", d, src.ap, reads=src.bufs)

    def V(l, off, n=1):
        return T(vecs[:, l, off:off + n], cst_b)

    class Region:
        def __init__(self):
            self.off = 0

        def reset(self, off=0):
            self.off = off

        def bf(self, n):
            a = self.off
            self.off += n
            assert self.off <= RN, ("region overflow", self.off, RN)
            return R[:, a:a + n]

        def f32(self, n):
            if self.off % 2:
                self.off += 1
            a = self.off
            self.off += 2 * n
            assert self.off <= RN, ("region overflow", self.off, RN)
            return R[:, a:a + 2 * n].bitcast(F32)

        def i32(self, n):
            if self.off % 2:
                self.off += 1
            a = self.off
            self.off += 2 * n
            assert self.off <= RN
            return R[:, a:a + 2 * n].bitcast(I32)

    reg = Region()

    class Ring:
        def __init__(self, aps):
            self.items = [T(a, Buf()) for a in aps]
            self.i = 0

        def next(self):
            t = self.items[self.i % len(self.items)]
            self.i += 1
            return t

    pend = [None]

    def _mk_dma(k):
        cell = {}
        op = P.add("pool", lambda e: e.dma_start(out=cell["out"], in_=cell["in"]), writes=[wst_b[k]], dma=True)
        return cell

    def wload(l, off, n):
        k = wst_rr[0] % 2
        wst_rr[0] += 1
        if skip[0]:
            return k
        if pend[0] is None:
            cell = _mk_dma(k)
        else:
            cell = pend[0]
        cell["out"] = wst[k][:, 0:n]
        cell["in"] = wpk_d[l, :, off:off + n]
        pend[0] = _mk_dma((k + 1) % 2)
        return k

    def wflush():
        if pend[0] is not None:
            k = wst_rr[0] % 2
            pend[0]["out"] = wst[k][:, 0:16]
            pend[0]["in"] = wpk_d[0, :, 0:16]
            pend[0] = None

    def wblk(k, off, kc, w):
        return T(wst[k][:, off:off + kc * w].rearrange("p (c w) -> p c w", c=kc), wst_b[k])

    def perm(ap, d, n0, w):
        L = S // d
        if d == 1:
            return ap[:, n0:n0 + w], 1
        v = ap.rearrange("p (l r) -> p r l", r=d)
        if w <= L:
            r = n0 // L
            l0 = n0 % L
            return v[:, r, l0:l0 + w], 1
        k = w // L
        r0 = n0 // L
        return v[:, r0:r0 + k, :], k

    def permw(ap, d, i):
        if d == 1:
            return ap[:, i * 512:(i + 1) * 512]
        n = 512 // d
        return ap.rearrange("p (r l) -> p l r", r=d)[:, i * n:(i + 1) * n, :]

    def shp2(ap, d):
        if d == 1:
            return ap
        return ap.rearrange("p (l r) -> p l r", r=d)

    def shp(ap, k):
        if k == 1:
            return ap
        return ap.rearrange("p (k l) -> p k l", k=k)

    dma("sp", ident[:, :], cst_d[:, C_ID:C_ID + 128], writes=[cst_b])
    dma("sp", trilp[:, :], cst_d[:, C_TRIL:C_TRIL + 128], writes=[cst_b])
    dma("sp", upp[:, :], cst_d[:, C_UPP:C_UPP + 128], writes=[cst_b])
    dma("sp", inv2[:, :], cst_d[:, C_INV:C_INV + 1], writes=[cst_b], slow=True)
    dma("pool", maskA[:, :], cst_d[:, C_MASK:C_MASK + 256], writes=[cst_b])
    dma("pool", rmat[:, :], cst_d[:, C_RMAT:C_RMAT + 128], writes=[cst_b])
    dma("pool", ident_bf[:, :], cst_d[:, C_ID:C_ID + 128], writes=[cst_b])
    dma("pool", mbias[:, :], cst_d[:, C_MBIAS:C_MBIAS + 256], writes=[cst_b])
    dma("sp", vecs[:, :, :], vec_d.rearrange("l p n -> p l n"), writes=[cst_b])
    memset("dve", T(ones_bf[:, :], cst_b), 1.0)
    memset("dve", T(ones32[:, :], cst_b), 1.0)
    memset("dve", T(epsT[:, :], cst_b), EPS)
    P.barrier()
    ONES = T(ones_bf[:, :], cst_b)
    MASKA = T(maskA[:, :], cst_b)
    MASKC = T(maskA[:, 0:128], cst_b)
    RMAT = T(rmat[:, :], cst_b)
    EPSB = T(epsT[:, :], cst_b)

    def xt(c, i, w=512):
        return T(xT[:, c, i * 512:i * 512 + w], xT_b[c][i])

    def rstd_from_ss(ss_ps, n_feat, tmp_ring, parts=128):
        t = tmp_ring.next()
        tp = sub(t, t.ap[0:parts, :])
        act(tp, sub(ss_ps, ss_ps.ap[0:parts, :]), AF.Ln, scale=1.0 / n_feat, bias=sub(EPSB, epsT[0:parts, :]))
        act(tp, tp, AF.Exp, scale=-0.5)
        return tp

    def rmsnorm_h(l, goff, hT, hT_b, tiles, f32ring, sqring):
        for k, i in enumerate(tiles):
            ss = bank()
            for c in range(8):
                sq = sqring.next()
                act(sq, xt(c, i), AF.Square)
                mm(ss, ONES, sq, start=(c == 0), stop=(c == 7))
            rs = rstd_from_ss(ss, D, f32ring)
            for c in range(8):
                stt("dve", T(hT[:, c, k * 512:(k + 1) * 512], hT_b[c][k]), xt(c, i), V(l, goff + c), rs, ALU.mult, ALU.mult)

    for sq_i in range(nseq):
        phase(0)
        P.barrier()
        reg.reset()
        set_pool(range(8))
        xin = Ring([reg.f32(1024) for _ in range(3)])
        for tt_i in range(16):
            xi = xin.next()
            dma("sp", xi.ap, x_d[sq_i, tt_i * 128:(tt_i + 1) * 128, :], writes=xi.bufs)
            for half in range(2):
                pb = bank()
                for cc in range(4):
                    c = half * 4 + cc
                    tr(sub(pb, pb.ap[:, cc * 128:(cc + 1) * 128]), sub(xi, xi.ap[:, c * 128:(c + 1) * 128]))
                dst = T(xT[:, half * 4:half * 4 + 4, tt_i * 128:(tt_i + 1) * 128],
                        [xT_b[half * 4 + cc][tt_i // 4] for cc in range(4)])
                cp("act" if half == 0 else "dve", dst, sub(pb, pb.ap.rearrange("p (c t) -> p c t", c=4)))
        phase(1)
        P.barrier()
        reg.reset()
        tA = T(reg.f32(S), Buf())
        tB = T(reg.f32(S), Buf())
        tC = T(reg.f32(S), Buf())
        for pi_ in range(4):
            dma("sp", posi_t[:, :], pos_d[sq_i, :, pi_ * 512:(pi_ + 1) * 512], writes=[posi_b])
            cp("pool", sub(tA, tA.ap[:, pi_ * 512:(pi_ + 1) * 512]), T(posi_t[:, :], posi_b))
        ts("dve", tA, tA, T(inv2[:, 0:1], cst_b), None, ALU.mult)
        MG = 12582912.0
        TAB = T(sinT[:, :], tab_b)
        TABC = T(cosT[:, :], tab_b)
        for which, shift, dst in ((0, 0.0, TAB), (1, 0.25, TABC)):
            ts("dve", tB, tA, 1.0 / (2 * math.pi), shift, ALU.mult, ALU.add)
            ts("dve", tC, tB, MG, -MG, ALU.add, ALU.add)
            tt("dve", tB, tB, tC, ALU.subtract)
            act(dst, tB, AF.Sin, scale=6.283185)

        for l in range(nlayers):
            phase(2)
            P.barrier()
            reg.reset()
            set_pool(range(8))
            hT = reg.bf(8 * S).rearrange("p (c t) -> p c t", c=8)
            hT_b = [[Buf() for _ in range(NT)] for _ in range(8)]
            mg_off = reg.off
            mg = reg.bf(8 * S).rearrange("p (c t) -> p c t", c=8)
            mg_b = [[Buf() for _ in range(NT)] for _ in range(8)]
            yT_off = reg.off
            yT = reg.bf(4 * S).rearrange("p (c t) -> p c t", c=4)
            yT_b = [[Buf() for _ in range(NT)] for _ in range(4)]
            base_off = reg.off
            f32ring = Ring([reg.f32(512) for _ in range(3)])
            sqring = Ring([reg.bf(512) for _ in range(3)])
            rmsnorm_h(l, V_PRE_MIX, hT, hT_b, list(range(NT)), f32ring, sqring)
            if sq_i == 0 and l == 0:
                tap("hT0", T(hT[:, 0, :], [hT_b[0][i] for i in range(NT)]))

            def hTt(c, i):
                return T(hT[:, c, i * 512:(i + 1) * 512], hT_b[c][i])

            def hT_all(c):
                return [hT_b[c][i] for i in range(NT)]

            def merge_pass(g, first):
                set_pool(range(8))
                for j in range(8):
                    k = wload(l, W_MRG + (g * 8 + j) * 1536, 1536)
                    wg = wblk(k, 0, 8, 128)
                    wb = wblk(k, 1024, 4, 128)
                    for i in range(NT):
                        gp = bank()
                        for c in range(8):
                            mm(gp, sub(wg, wg.ap[:, c, :]), hTt(c, i), start=(c == 0), stop=(c == 7))
                        pp = bank()
                        for c in range(4):
                            mm(pp, sub(wb, wb.ap[:, c, :]), T(yT[:, c, i * 512:(i + 1) * 512], yT_b[c][i]),
                               start=(c == 0), stop=(c == 3))
                        sg = mtmp.next()
                        act(sg, gp, AF.Sigmoid, bias=V(l, V_BGATE + g * 8 + j))
                        mt = T(mg[:, j, i * 512:(i + 1) * 512], mg_b[j][i])
                        if first:
                            tt("dve", mt, sg, pp, ALU.mult)
                        else:
                            sgb = T(sg.ap.bitcast(BF16)[:, 0:512], sg.bufs)
                            tt("dve", sgb, sg, pp, ALU.mult)
                            tt("pool", mt, mt, sgb, ALU.add)

            phase(3)
            P.barrier()
            reg.reset(base_off)
            oacc = reg.f32(S)
            dacc = reg.f32(S)
            oacc_b = Buf()
            dacc_b = Buf()
            pring = Ring([reg.bf(256) for _ in range(4)])
            qsb = Ring([reg.bf(512) for _ in range(2)])
            usb = Ring([reg.bf(512) for _ in range(2)])
            t1r = Ring([reg.f32(512) for _ in range(3)])
            mtmp = t1r
            att_end = reg.off
            reg.reset(mg_off)
            qkv = []
            for _ in range(2):
                qkv.append(dict(q=reg.bf(S), k=reg.bf(S), v=reg.bf(S),
                                qb=[Buf() for _ in range(NT)], kb=[Buf() for _ in range(NT)],
                                vb=[Buf() for _ in range(NT)]))
            vts = reg.bf(S)
            vts_b = [Buf() for _ in range(NT)]
            assert reg.off <= yT_off
            reg.reset(att_end)
            scale = 128.0 ** -0.5
            units = [(h, g) for h in range(4) for g in GSET]
            set_pool([0, 1, 2], "b")
            set_pool([3, 4, 5], "a")
            odbank = [T(ps[6][:, :], ps_b[6]), T(ps[7][:, :], ps_b[7])]

            def proj_gen(u):
                h, g = units[u]
                bset = qkv[u % 2]
                d = DILS[g]
                k = wload(l, W_ATT + (g * 4 + h) * 3072, 3072)
                wq = wblk(k, 0, 8, 128)
                wk = wblk(k, 1024, 8, 128)
                wv = wblk(k, 2048, 8, 128)
                for (w_, dst, dst_b) in ((wq, bset["q"], bset["qb"]), (wk, bset["k"], bset["kb"])):
                    pendR = None
                    for i in range(NT + 1):
                        if i < NT:
                            pp = bank("a")
                            for c in range(8):
                                mm(pp, sub(w_, w_.ap[:, c, :]), hTt(c, i), start=(c == 0), stop=(c == 7))
                            qs = qsb.next()
                            cp("act", qs, pp)
                            t1 = t1r.next()
                            tt("pool", t1, qs, T(cosT[:, i * 512:(i + 1) * 512], tab_b), ALU.mult)
                            us = usb.next()
                            tt("pool", us, qs, T(sinT[:, i * 512:(i + 1) * 512], tab_b), ALU.mult)
                        if pendR is not None:
                            (pus, pt1, pi) = pendR
                            rp = bank("a")
                            mm(rp, RMAT, pus)
                            wb_ = [dst_b[pi]] if d == 1 else list(dst_b)
                            tt("dve", T(permw(dst, d, pi), wb_), sub(pt1, shp2(pt1.ap, d)), sub(rp, shp2(rp.ap, d)), ALU.add)
                        pendR = (us, t1, i) if i < NT else None
                        yield
                for i in range(NT):
                    pp = bank("a")
                    for c in range(8):
                        mm(pp, sub(wv, wv.ap[:, c, :]), hTt(c, i), start=(c == 0), stop=(c == 7))
                    wb_ = [vts_b[i]] if d == 1 else list(vts_b)
                    cp("act", T(permw(vts, d, i), wb_), sub(pp, shp2(pp.ap, d)))
                    yield
                for i in range(NT):
                    pp = bank("a")
                    for bb in range(4):
                        B = i * 4 + bb
                        mm(sub(pp, pp.ap[:, bb * 128:(bb + 1) * 128]), T(vts[:, B * 128:(B + 1) * 128], vts_b[i]),
                           T(ident_bf[:, :], cst_b))
                    cp("act", T(bset["v"][:, i * 512:(i + 1) * 512], bset["vb"][i]), pp)
                    yield

            def core_gen(u):
                h, g = units[u]
                bset = qkv[u % 2]
                qT, kT, vtk = bset["q"], bset["k"], bset["v"]
                qT_b, kT_b, v_b = bset["qb"], bset["kb"], bset["vb"]
                d = DILS[g]
                L = S // d
                nb = L // 128
                first = (g == GSET[0])
                last = (g == GSET[-1])
                pt = {}

                def scores(B):
                    n = B % nb
                    wq_ = 256 if n + 1 < nb else 128
                    sp_ = bank("b")
                    rb = [qT_b[(B * 128) // 512]]
                    if wq_ == 256 and ((B + 1) * 128) // 512 != (B * 128) // 512:
                        rb.append(qT_b[((B + 1) * 128) // 512])
                    mm(sub(sp_, sp_.ap[:, 0:wq_]), T(kT[:, B * 128:(B + 1) * 128], kT_b[B // 4]),
                       T(qT[:, B * 128:B * 128 + wq_], rb), start=True, stop=False)
                    mm(sub(sp_, sp_.ap[:, 0:wq_]), T(ident_bf[:, :], cst_b), T(mbias[:, 0:wq_], cst_b), start=False, stop=True)
                    p_ = pring.next()
                    act(sub(p_, p_.ap[:, 0:wq_]), sub(sp_, sp_.ap[:, 0:wq_]), AF.Exp, scale=scale)
                    pt[B] = p_

                scores(0)
                yield
                for B in range(16):
                    if B + 1 < 16:
                        scores(B + 1)
                    n = B % nb
                    i2 = B // 2
                    ob = odbank[i2 % 2]
                    osl = sub(ob, ob.ap[:, (B % 2) * 128:(B % 2 + 1) * 128])
                    dsl = sub(ob, ob.ap[:, 256 + (B % 2) * 128:256 + (B % 2 + 1) * 128])
                    vcur = T(vtk[:, B * 128:(B + 1) * 128], v_b[B // 4])
                    pc = sub(pt[B], pt[B].ap[:, 0:128])
                    if n > 0:
                        vprev = T(vtk[:, (B - 1) * 128:B * 128], v_b[(B - 1) // 4])
                        ppv = sub(pt[B - 1], pt[B - 1].ap[:, 128:256])
                        mm(osl, vprev, ppv, start=True, stop=False)
                        mm(osl, vcur, pc, start=False, stop=True)
                        mm(dsl, ONES, ppv, start=True, stop=False)
                        mm(dsl, ONES, pc, start=False, stop=True)
                    else:
                        mm(osl, vcur, pc, start=True, stop=True)
                        mm(dsl, ONES, pc, start=True, stop=True)
                    if B % 2 == 1:
                        ov, kk_ = perm(oacc, d, i2 * 256, 256)
                        dv, _ = perm(dacc, d, i2 * 256, 256)
                        OA = T(ov, oacc_b)
                        DA = T(dv, dacc_b)
                        o_src = sub(ob, shp(ob.ap[:, 0:256], kk_))
                        d_src = sub(ob, shp(ob.ap[:, 256:512], kk_))
                        if first:
                            cp("act", OA, o_src)
                            cp("act", DA, d_src)
                        else:
                            tt("dve", OA, OA, o_src, ALU.add)
                            tt("dve", DA, DA, d_src, ALU.add)
                    yield
                if last:
                    DAF = T(dacc, dacc_b)
                    recip(DAF, DAF)
                    tt("dve", T(yT[:, h, :], [yT_b[h][i] for i in range(NT)]), T(oacc, oacc_b), DAF, ALU.mult)
                    yield

            def drive(ga, gb):
                ga = iter(ga) if ga is not None else None
                gb = iter(gb) if gb is not None else None
                while ga is not None or gb is not None:
                    if ga is not None:
                        try:
                            next(ga)
                        except StopIteration:
                            ga = None
                    if gb is not None:
                        try:
                            next(gb)
                        except StopIteration:
                            gb = None

            phase(3)
            drive(proj_gen(0), None)
            for u in range(len(units)):
                drive(proj_gen(u + 1) if u + 1 < len(units) else None, core_gen(u))
            set_pool(range(8), "a")
            if sq_i == 0 and l == 0:
                tap("ydil", T(yT[:, 0, :], [yT_b[0][i] for i in range(NT)]))
            phase(4)
            merge_pass(2, True)

            phase(5)
            P.barrier()
            reg.reset(base_off)
            set_pool(range(8))
            l1 = reg.f32(16 * 256).rearrange("p (n f) -> p n f", n=16)
            l1_b = [Buf() for _ in range(16)]
            gla_off = reg.off
            alow = reg.f32(S)
            alow_b = [Buf() for _ in range(NT)]
            e1r = Ring([reg.f32(512) for _ in range(2)])
            wups = reg.f32(512)
            wups_b = Buf()
            dma("sp", wups[0:32, :], wup_d[l, :, :], writes=[wups_b])
            k = wload(l, W_ALOW, 256)
            wa = wblk(k, 0, 8, 32)
            for i in range(NT):
                pp = bank()
                for c in range(8):
                    mm(sub(pp, pp.ap[0:32, :]), sub(wa, wa.ap[:, c, :]), hTt(c, i), start=(c == 0), stop=(c == 7))
                cp("act", T(alow[0:32, i * 512:(i + 1) * 512], alow_b[i]), sub(pp, pp.ap[0:32, :]))
            WUP = T(wups[0:32, 0:256], wups_b)
            BVEC = T(wups[0:32, 256:512], wups_b)
            O32 = T(ones32[0:32, :], cst_b)
            for n2 in range(8):
                pp = bank()
                for u in range(2):
                    n = n2 * 2 + u
                    o_ = sub(pp, pp.ap[:, u * 256:(u + 1) * 256])
                    mm(o_, T(alow[0:32, n * 128:(n + 1) * 128], alow_b[n // 4]), WUP, start=True, stop=False)
                    mm(o_, O32, BVEC, start=False, stop=True)
                e1 = e1r.next()
                act(e1, pp, AF.Exp, scale=-1.0)
                act(T(l1[:, n2 * 2:n2 * 2 + 2, :], [l1_b[n2 * 2], l1_b[n2 * 2 + 1]]),
                    sub(e1, e1.ap.rearrange("p (n f) -> p n f", n=2)), AF.Ln, bias=1.0)
            if sq_i == 0 and l == 0:
                tap("l1", T(l1[:, 0, :], l1_b[0]))
            P.barrier()
            reg.reset(gla_off)
            phase(6)
            qqr = Ring([reg.bf(512) for _ in range(2)])
            kkr = Ring([reg.bf(512) for _ in range(2)])
            vhr = Ring([reg.bf(512) for _ in range(2)])
            kinr = Ring([reg.bf(256) for _ in range(2)])
            gsr = Ring([reg.f32(512) for _ in range(1)])
            eLr = Ring([reg.f32(512) for _ in range(1)])
            enLr = Ring([reg.f32(512) for _ in range(1)])
            eRr = Ring([reg.f32(256) for _ in range(1)])
            dec = T(reg.f32(16), Buf())
            Sst = T(reg.f32(128), Buf())
            Sbf = T(reg.bf(128), Buf())
            pmr = Ring([reg.bf(128) for _ in range(2)])
            osq = Ring([reg.bf(512) for _ in range(1)])
            y1r = Ring([reg.f32(512) for _ in range(2)])
            mtmp = y1r
            TRIL = T(trilp[:, :], cst_b)
            UPP = T(upp[:, :], cst_b)
            obank = [T(ps[6][:, :], ps_b[6]), T(ps[7][:, :], ps_b[7])]
            set_pool(range(6))
            for h in range(4):
                k = wload(l, W_GLA + h * 3072, 3072)
                wh = wblk(k, 0, 8, 384)
                memset("dve", sub(Sst, Sst.ap[0:64, :]), 0.0)
                for i in range(NT):
                    lt = bank()
                    for u in range(4):
                        n = i * 4 + u
                        mm(sub(lt, lt.ap[0:64, u * 128:(u + 1) * 128]), T(l1[:, n, h * 64:(h + 1) * 64], l1_b[n]), TRIL)
                    eL = eLr.next()
                    enL = enLr.next()
                    act(sub(eL, eL.ap[0:64, :]), sub(lt, lt.ap[0:64, :]), AF.Exp)
                    act(sub(enL, enL.ap[0:64, :]), sub(lt, lt.ap[0:64, :]), AF.Exp, scale=-1.0)
                    cp("pool", sub(dec, dec.ap[0:64, i * 4:i * 4 + 4]),
                       sub(eL, eL.ap[0:64, :].rearrange("p (u t) -> p u t", u=4)[:, :, 127]))
                    pq = bank()
                    for c in range(8):
                        mm(sub(pq, pq.ap[0:64, :]), sub(wh, wh.ap[:, c, 0:64]), hTt(c, i), start=(c == 0), stop=(c == 7))
                    qq = qqr.next()
                    stt("dve", sub(qq, qq.ap[0:64, :]), sub(pq, pq.ap[0:64, :]), 0.125,
                        sub(eL, eL.ap[0:64, :]), ALU.mult, ALU.mult)
                    pk = bank()
                    for c in range(8):
                        mm(sub(pk, pk.ap[0:64, :]), sub(wh, wh.ap[:, c, 64:128]), hTt(c, i), start=(c == 0), stop=(c == 7))
                    kk = kkr.next()
                    tt("dve", sub(kk, kk.ap[0:64, :]), sub(pk, pk.ap[0:64, :]), sub(enL, enL.ap[0:64, :]), ALU.mult)
                    pg = bank()
                    for c in range(8):
                        mm(pg, sub(wh, wh.ap[:, c, 256:384]), hTt(c, i), start=(c == 0), stop=(c == 7))
                    gs = gsr.next()
                    act(gs, pg, AF.Silu)
                    lr = bank()
                    pkt = bank()
                    for u in range(4):
                        n = i * 4 + u
                        mm(sub(lr, lr.ap[:, u * 64:(u + 1) * 64]), UPP, T(l1[:, n, h * 64:(h + 1) * 64], l1_b[n]))
                        for c in range(8):
                            hv = T(hT[:, c, n * 128:(n + 1) * 128], hT_b[c][i])
                            mm(sub(pkt, pkt.ap[:, u * 64:(u + 1) * 64]), hv, sub(wh, wh.ap[:, c, 64:128]),
                               start=(c == 0), stop=(c == 7))
                    eR = eRr.next()
                    act(eR, sub(lr, lr.ap[:, 0:256]), AF.Exp)
                    kin = kinr.next()
                    tt("dve", kin, sub(pkt, pkt.ap[:, 0:256]), eR, ALU.mult)
                    pvt = bank()
                    for u in range(4):
                        n = i * 4 + u
                        for c in range(8):
                            hv = T(hT[:, c, n * 128:(n + 1) * 128], hT_b[c][i])
                            mm(sub(pvt, pvt.ap[:, u * 128:(u + 1) * 128]), hv, sub(wh, wh.ap[:, c, 128:256]),
                               start=(c == 0), stop=(c == 7))
                    vh = vhr.next()
                    cp("act", vh, pvt)
                    ob = obank[i % 2]
                    for u in range(4):
                        n = i * 4 + u
                        qqc = sub(qq, qq.ap[0:64, u * 128:(u + 1) * 128])
                        kkc = sub(kk, kk.ap[0:64, u * 128:(u + 1) * 128])
                        vc = sub(vh, vh.ap[:, u * 128:(u + 1) * 128])
                        sp_ = bank()
                        mm(sub(sp_, sp_.ap[:, 0:128]), kkc, qqc)
                        pm = pmr.next()
                        stt("dve", pm, sub(sp_, sp_.ap[:, 0:128]), -16.0, TRIL, ALU.mult, ALU.mult)
                        osl = sub(ob, ob.ap[:, u * 128:(u + 1) * 128])
                        if n == 0:
                            mm(osl, vc, pm, start=True, stop=True)
                        else:
                            mm(osl, vc, pm, start=True, stop=False)
                            mm(osl, sub(Sbf, Sbf.ap[0:64, :]), qqc, start=False, stop=True)
                        if n < 15:
                            up_ = bank()
                            mm(sub(up_, up_.ap[0:64, 0:128]), sub(kin, kin.ap[:, u * 64:(u + 1) * 64]), vc)
                            if n == 0:
                                cp("dve", sub(Sst, Sst.ap[0:64, :]), sub(up_, up_.ap[0:64, 0:128]))
                            else:
                                stt("dve", sub(Sst, Sst.ap[0:64, :]), sub(Sst, Sst.ap[0:64, :]),
                                    sub(dec, dec.ap[0:64, n:n + 1]), sub(up_, up_.ap[0:64, 0:128]), ALU.mult, ALU.add)
                            cp("act", sub(Sbf, Sbf.ap[0:64, :]), sub(Sst, Sst.ap[0:64, :]))
                    o2 = osq.next()
                    act(o2, ob, AF.Square)
                    ss = bank()
                    mm(ss, ONES, o2)
                    rs = rstd_from_ss(ss, 128, y1r)
                    y1 = y1r.next()
                    stt("dve", y1, ob, V(l, V_GLAG), rs, ALU.mult, ALU.mult)
                    tt("dve", T(yT[:, h, i * 512:(i + 1) * 512], yT_b[h][i]), y1, gs, ALU.mult)
            if sq_i == 0 and l == 0:
                tap("ygla", T(yT[:, 0, :], [yT_b[0][i] for i in range(NT)]))
            phase(7)
            merge_pass(0, False)

            phase(8)
            P.barrier()
            reg.reset(base_off)
            set_pool(range(8))
            ub = reg.f32(S + 2)
            ub_b = Buf()
            xsr = Ring([reg.f32(512) for _ in range(2)])
            accr = Ring([reg.f32(512) for _ in range(2)])
            mtmp = Ring([reg.f32(512) for _ in range(2)])
            memset("dve", T(ub[:, 0:2], ub_b), 0.0)
            wmx_off = reg.off
            wmx = reg.bf(8192).rearrange("p (a b w) -> p a b w", a=8, b=8)
            wmx_b = Buf()
            for a in range(4):
                dma("pool", wmx[:, 2 * a:2 * a + 2, :, :].rearrange("p a b w -> p (a b w)"),
                    wpk_d[l, :, W_MIX + a * 2048:W_MIX + (a + 1) * 2048], writes=[wmx_b])
            for ch in range(4):
                k = wload(l, W_SC + ch * 3072, 3072)
                wb_ = wblk(k, 0, 8, 128)
                wc_ = wblk(k, 1024, 8, 128)
                wx_ = wblk(k, 2048, 8, 128)
                for i in range(NT):
                    pb_ = bank()
                    pc_ = bank()
                    px_ = bank()
                    for (w_, p_) in ((wb_, pb_), (wc_, pc_), (wx_, px_)):
                        for c in range(8):
                            mm(p_, sub(w_, w_.ap[:, c, :]), hTt(c, i), start=(c == 0), stop=(c == 7))
                    xs = xsr.next()
                    cp("act", xs, px_)
                    UB = T(ub[:, 2 + i * 512:2 + (i + 1) * 512], ub_b)
                    tt("dve", UB, pc_, xs, ALU.mult)
                    acc = accr.next()
                    act(acc, UB, AF.Copy, scale=vecs[:, l, V_SCCW + 2 * 4 + ch:V_SCCW + 2 * 4 + ch + 1])
                    stt("dve", acc, T(ub[:, 1 + i * 512:1 + (i + 1) * 512], ub_b), V(l, V_SCCW + 1 * 4 + ch), acc, ALU.mult, ALU.add)
                    stt("dve", acc, T(ub[:, i * 512:(i + 1) * 512], ub_b), V(l, V_SCCW + 0 * 4 + ch), acc, ALU.mult, ALU.add)
                    tt("dve", T(yT[:, ch, i * 512:(i + 1) * 512], yT_b[ch][i]), pb_, acc, ALU.mult)
            if sq_i == 0 and l == 0:
                tap("ysc", T(yT[:, 0, :], [yT_b[0][i] for i in range(NT)]))
            merge_pass(1, False)
            if sq_i == 0 and l == 0:
                tap("mg", T(mg[:, 0, :], [mg_b[0][i] for i in range(NT)]))

            phase(9)
            P.barrier()
            reg.reset(yT_off)
            set_pool(range(8))
            mo2 = [R[:, kk2 * 8192:(kk2 + 1) * 8192].bitcast(F32).rearrange("p (c t) -> p c t", c=8) for kk2 in range(2)]
            mo2_b = [[Buf() for _ in range(8)] for _ in range(2)]
            sqr = Ring([reg.bf(512) for _ in range(3)])
            rsr = Ring([reg.f32(512) for _ in range(2)])
            tmr = Ring([reg.f32(512) for _ in range(3)])
            assert reg.off <= wmx_off, (reg.off, wmx_off)
            set_pool(range(6))
            ssm = [T(ps[6][:, :], ps_b[6]), T(ps[7][:, :], ps_b[7])]
            pendE = None

            def mix_epi(pe_, jo):
                (ei, ers, emo, emo_b) = pe_
                tm = tmr.next()
                stt("dve", tm, T(emo[:, jo, :], emo_b[jo]), V(l, V_POST_MIX + jo), ers, ALU.mult, ALU.mult)
                tt("dve", xt(jo, ei), xt(jo, ei), tm, ALU.add)

            for i in range(NT):
                ss = ssm[i % 2]
                mo = mo2[i % 2]
                mo_b = mo2_b[i % 2]
                pendS = None
                for jo in range(8):
                    pp = bank()
                    for j in range(8):
                        mm(pp, T(wmx[:, jo, j, :], wmx_b), T(mg[:, j, i * 512:(i + 1) * 512], mg_b[j][i]),
                           start=(j == 0), stop=(j == 7))
                    cp("dve", T(mo[:, jo, :], mo_b[jo]), pp)
                    s2 = sqr.next()
                    act(s2, T(mo[:, jo, :], mo_b[jo]), AF.Square)
                    if pendS is not None:
                        mm(ss, ONES, pendS[0], start=(pendS[1] == 0), stop=False)
                    pendS = (s2, jo)
                    if pendE is not None:
                        mix_epi(pendE, jo)
                mm(ss, ONES, pendS[0], start=False, stop=True)
                rs = rstd_from_ss(ss, D, rsr)
                pendE = (i, rs, mo, mo_b)
            for jo in range(8):
                mix_epi(pendE, jo)
            if sq_i == 0 and l == 0:
                tap("xmid", T(xT[:, 0, :], [xT_b[0][i] for i in range(NT)]))

            phase(10)
            P.barrier()
            reg.reset()
            hal = reg.f32(NFF * 2).rearrange("p (f t) -> p f t", f=NFF)
            hal_b = Buf()
            ffbase = reg.off
            P.barrier()
            reg.reset(ffbase)
            set_pool(range(6))
            hTh = reg.bf(8 * 1024).rearrange("p (c t) -> p c t", c=8)
            hTh_b = [[Buf() for _ in range(2)] for _ in range(8)]
            hid = reg.bf(NFF * 1024).rearrange("p (f t) -> p f t", f=NFF)
            hid_b = [[Buf() for _ in range(2)] for _ in range(NFF)]
            yo = reg.f32(8 * 1024).rearrange("p (c t) -> p c t", c=8)
            yo_b = [[Buf() for _ in range(2)] for _ in range(8)]
            gbr = [(reg.f32(1026), Buf()) for _ in range(2)]
            accr = Ring([reg.f32(512) for _ in range(2)])
            ger = Ring([reg.f32(512) for _ in range(2)])
            f32ring = Ring([reg.f32(512) for _ in range(2)])
            sqring = Ring([reg.bf(512) for _ in range(3)])
            ssb = [T(ps[6][:, :], ps_b[6]), T(ps[7][:, :], ps_b[7])]

            def ffn_epilogue(half):
                for i in range(2):
                    rs = rstd_from_ss(ssb[i], D, f32ring)
                    gi = half * 2 + i
                    for j in range(8):
                        tm = ger.next()
                        stt("dve", tm, T(yo[:, j, i * 512:(i + 1) * 512], yo_b[j][i]), V(l, V_POST_FFN + j), rs, ALU.mult, ALU.mult)
                        tt("dve", xt(j, gi), xt(j, gi), tm, ALU.add)

            for half in range(2):
                rmsnorm_h(l, V_PRE_FFN, hTh, hTh_b, [half * 2, half * 2 + 1], f32ring, sqring)
                if half == 1:
                    ffn_epilogue(0)
                for f in range(NFF):
                    k = wload(l, W_FF + f * 2048, 2048)
                    wg = wblk(k, 0, 8, 128)
                    wu = wblk(k, 1024, 8, 128)
                    gb, gb_b = gbr[f % 2]
                    if half == 0:
                        memset("dve", T(gb[:, 0:2], gb_b), 0.0)
                    else:
                        cp("dve", T(gb[:, 0:2], gb_b), T(hal[:, f, :], hal_b))
                    for i in range(2):
                        pg = bank()
                        pu = bank()
                        for (w_, p_) in ((wg, pg), (wu, pu)):
                            for c in range(8):
                                mm(p_, sub(w_, w_.ap[:, c, :]), T(hTh[:, c, i * 512:(i + 1) * 512], hTh_b[c][i]),
                                   start=(c == 0), stop=(c == 7))
                        GB = T(gb[:, 2 + i * 512:2 + (i + 1) * 512], gb_b)
                        cp("act", GB, pg)
                        acc = accr.next()
                        act(acc, pg, AF.Identity, scale=vecs[:, l, V_FFCW + 2 * NFF + f:V_FFCW + 2 * NFF + f + 1],
                            bias=V(l, V_FFCB + f))
                        stt("dve", acc, T(gb[:, 1 + i * 512:1 + (i + 1) * 512], gb_b), V(l, V_FFCW + 1 * NFF + f), acc, ALU.mult, ALU.add)
                        stt("dve", acc, T(gb[:, i * 512:(i + 1) * 512], gb_b), V(l, V_FFCW + 0 * NFF + f), acc, ALU.mult, ALU.add)
                        ge = ger.next()
                        act(ge, acc, AF.Gelu_apprx_tanh)
                        tt("dve", T(hid[:, f, i * 512:(i + 1) * 512], hid_b[f][i]), ge, pu, ALU.mult)
                    if half == 0:
                        cp("dve", T(hal[:, f, :], hal_b), T(gb[:, 1024:1026], gb_b))
                pendS = None
                for j in range(8):
                    k = wload(l, W_DN + j * 2816, 2816)
                    wd = wblk(k, 0, NFF, 128)
                    for i in range(2):
                        pp = bank()
                        for f in range(NFF):
                            mm(pp, sub(wd, wd.ap[:, f, :]), T(hid[:, f, i * 512:(i + 1) * 512], hid_b[f][i]),
                               start=(f == 0), stop=(f == NFF - 1))
                        cp("dve", T(yo[:, j, i * 512:(i + 1) * 512], yo_b[j][i]), pp)
                        s2 = sqring.next()
                        act(s2, T(yo[:, j, i * 512:(i + 1) * 512], yo_b[j][i]), AF.Square)
                        if pendS is not None:
                            mm(ssb[pendS[2]], ONES, pendS[0], start=(pendS[1] == 0), stop=(pendS[1] == 7))
                        pendS = (s2, j, i)
                mm(ssb[pendS[2]], ONES, pendS[0], start=(pendS[1] == 0), stop=(pendS[1] == 7))
            ffn_epilogue(1)
            set_pool(range(8))
            if sq_i == 0 and l == 0:
                tap("xl0", T(xT[:, 0, :], [xT_b[0][i] for i in range(NT)]))

        phase(0)
        P.barrier()
        reg.reset()
        set_pool(range(8))
        oring = Ring([reg.f32(1024) for _ in range(3)])
        for tt_i in range(16):
            ot = oring.next()
            for half in range(2):
                pb = bank()
                for cc in range(4):
                    c = half * 4 + cc
                    tr(sub(pb, pb.ap[:, cc * 128:(cc + 1) * 128]),
                       T(xT[:, c, tt_i * 128:(tt_i + 1) * 128], xT_b[c][tt_i // 4]))
                cp("act" if half == 0 else "dve", sub(ot, ot.ap[:, half * 512:(half + 1) * 512]), pb)
            dma("sp", out_d[sq_i, tt_i * 128:(tt_i + 1) * 128, :], ot.ap, reads=ot.bufs)

    wflush()
    fin = P.add("sp", None)
    fin.deps = [(op, "raw") for op in P.dma_last.values()]
    stats = P.emit(nc, st)
    print("ops/waits per engine", stats)
    st.close()
    return nc


def _blk(Wc):
    K, w = Wc.shape
    return np.ascontiguousarray(Wc.reshape(K // 128, 128, w).transpose(1, 0, 2)).reshape(128, -1)


def pack_weights(w_in, w_gate, w_branch, w_mix_out, w_ff_gate, w_ff_up, w_ff_down):
    Ls = w_in.shape[0]
    out = np.zeros((Ls, 128, W_TOT), np.float32)
    for l in range(Ls):
        wi = w_in[l]
        o = out[l]
        for g in range(3):
            for h in range(4):
                b = W_ATT + (g * 4 + h) * 3072
                cq = (g * 4 + h) * 128
                o[:, b:b + 1024] = _blk(wi[:, O_DQ + cq:O_DQ + cq + 128])
                o[:, b + 1024:b + 2048] = _blk(wi[:, O_DK + cq:O_DK + cq + 128])
                o[:, b + 2048:b + 3072] = _blk(wi[:, O_DV + cq:O_DV + cq + 128])
        for h in range(4):
            cat = np.concatenate([wi[:, O_GQ + h * 64:O_GQ + (h + 1) * 64], wi[:, O_GK + h * 64:O_GK + (h + 1) * 64],
                                  wi[:, O_GV + h * 128:O_GV + (h + 1) * 128], wi[:, O_GO + h * 128:O_GO + (h + 1) * 128]], axis=1)
            o[:, W_GLA + h * 3072:W_GLA + (h + 1) * 3072] = _blk(cat)
        al = np.zeros((D, 32), np.float32)
        al[:, 0:16] = wi[:, O_AL:O_AL + 16]
        o[:, W_ALOW:W_ALOW + 256] = _blk(al)
        for ch in range(4):
            b = W_SC + ch * 3072
            o[:, b:b + 1024] = _blk(wi[:, O_SB + ch * 128:O_SB + (ch + 1) * 128])
            o[:, b + 1024:b + 2048] = _blk(wi[:, O_SC + ch * 128:O_SC + (ch + 1) * 128])
            o[:, b + 2048:b + 3072] = _blk(wi[:, O_SX + ch * 128:O_SX + (ch + 1) * 128])
        for g in range(3):
            for j in range(8):
                b = W_MRG + (g * 8 + j) * 1536
                o[:, b:b + 1024] = _blk(w_gate[l][:, g * D + j * 128:g * D + (j + 1) * 128])
                o[:, b + 1024:b + 1536] = _blk(w_branch[l][g][:, j * 128:(j + 1) * 128])
        for jo in range(8):
            o[:, W_MIX + jo * 1024:W_MIX + (jo + 1) * 1024] = _blk(w_mix_out[l][:, jo * 128:(jo + 1) * 128])
        for f in range(NFF):
            b = W_FF + f * 2048
            o[:, b:b + 1024] = _blk(w_ff_gate[l][:, f * 128:(f + 1) * 128])
            o[:, b + 1024:b + 2048] = _blk(w_ff_up[l][:, f * 128:(f + 1) * 128])
        for j in range(8):
            o[:, W_DN + j * 2816:W_DN + (j + 1) * 2816] = _blk(w_ff_down[l][:, j * 128:(j + 1) * 128])
    return out


def pack_vecs(b_gate, pre_mix_g, post_mix_g, pre_ffn_g, post_ffn_g, ff_conv_w, ff_conv_b, sc_conv_w, gla_norm_g):
    Ls = b_gate.shape[0]
    v = np.zeros((Ls, 128, NV), np.float32)

    def col(a):
        return a.reshape(-1, 128).T

    for l in range(Ls):
        v[l, :, V_PRE_MIX:V_PRE_MIX + 8] = col(pre_mix_g[l])
        v[l, :, V_POST_MIX:V_POST_MIX + 8] = col(post_mix_g[l])
        v[l, :, V_PRE_FFN:V_PRE_FFN + 8] = col(pre_ffn_g[l])
        v[l, :, V_POST_FFN:V_POST_FFN + 8] = col(post_ffn_g[l])
        v[l, :, V_BGATE:V_BGATE + 24] = col(b_gate[l])
        for k in range(3):
            v[l, :, V_FFCW + k * NFF:V_FFCW + (k + 1) * NFF] = col(ff_conv_w[l, k])
            v[l, :, V_SCCW + k * 4:V_SCCW + (k + 1) * 4] = col(sc_conv_w[l, k])
        v[l, :, V_FFCB:V_FFCB + NFF] = col(ff_conv_b[l])
        v[l, :, V_GLAG] = gla_norm_g[l]
    return v


def make_consts():
    c = np.zeros((128, NCONST), np.float32)
    c[:, C_ID:C_ID + 128] = np.eye(128, dtype=np.float32)
    j = np.arange(128)[:, None]
    i = np.arange(128)[None, :]
    c[:, C_TRIL:C_TRIL + 128] = np.where(j <= i, -1.0 / 16.0, 0.0)
    c[:, C_UPP:C_UPP + 128] = np.where(j > i, -1.0 / 16.0, 0.0)
    c[:, C_MASK:C_MASK + 128] = (j <= i)
    c[:, C_MASK + 128:C_MASK + 256] = (j >= i)
    r = np.zeros((128, 128), np.float32)
    for m in range(64):
        r[m + 64, m] = -1.0
        r[m, m + 64] = 1.0
    c[:, C_RMAT:C_RMAT + 128] = r
    inv = (np.float32(10000.0) ** (-(np.arange(0, 128, 2, dtype=np.float32) / np.float32(128.0)))).astype(np.float32)
    c[:, C_INV] = np.concatenate([inv, inv])
    c[:, C_MBIAS:C_MBIAS + 256] = np.where(c[:, C_MASK:C_MASK + 256] > 0.5, 0.0, -30000.0)
    return c


_NC_CACHE = {}


def prepare_shared(w_in, w_alpha_up, b_alpha, gla_norm_g, sc_conv_w, w_gate, b_gate, w_branch, w_mix_out,
                   pre_mix_g, post_mix_g, pre_ffn_g, post_ffn_g, w_ff_gate, w_ff_up, ff_conv_w, ff_conv_b, w_ff_down):
    f = lambda a: np.asarray(a, np.float32)
    wpk = pack_weights(f(w_in), f(w_gate), f(w_branch), f(w_mix_out), f(w_ff_gate), f(w_ff_up), f(w_ff_down))
    vecs = pack_vecs(f(b_gate), f(pre_mix_g), f(post_mix_g), f(pre_ffn_g), f(post_ffn_g), f(ff_conv_w),
                     f(ff_conv_b), f(sc_conv_w), f(gla_norm_g))
    Ls = wpk.shape[0]
    wup = np.zeros((Ls, 32, 512), np.float32)
    wup[:, 0:16, 0:256] = f(w_alpha_up)
    wup[:, 0, 256:512] = f(b_alpha)
    return dict(wpk=wpk, vecs=vecs, wup=wup, consts=make_consts())


def kernel(x, positions, **weights):
    x = np.asarray(x, np.float32)
    positions = np.asarray(positions, np.int32)
    shared = prepare_shared(**weights)
    n_cores = 8
    per = x.shape[0] // n_cores
    if "nc" not in _NC_CACHE:
        _NC_CACHE["nc"] = build(nseq=per, nlayers=2)
    nc = _NC_CACHE["nc"]
    in_maps = []
    for c in range(n_cores):
        xs = np.ascontiguousarray(x[c * per:(c + 1) * per])
        pb = np.ascontiguousarray(np.broadcast_to(positions[c * per:(c + 1) * per, None, :], (per, 128, S)))
        m = dict(x=xs, posb=pb)
        m.update(shared)
        in_maps.append(m)
    res = run_bass_kernel_spmd(nc, in_maps, core_ids=list(range(n_cores)))
    return np.concatenate([r["out"] for r in res.results], axis=0)
```

```python
import math
from contextlib import ExitStack
import numpy as np
import concourse.bass as bass
import concourse.mybir as mybir
from concourse.bass_utils import run_bass_kernel_spmd

F32 = mybir.dt.float32
BF16 = mybir.dt.bfloat16
I32 = mybir.dt.int32
AF = mybir.ActivationFunctionType
ALU = mybir.AluOpType

PHASE_LOG = []
ENGS = ["pe", "act", "dve", "pool", "sp"]
N_DMA_SEMS = 8

S = 2048
D = 1024
NT = 4
DFF = 2816
NFF = 22
EPS = 1e-6
DILS = (1, 4, 16)
import os
GSET = [int(c) for c in os.environ.get('ATT_G', '012')]
ATT_LVL = int(os.environ.get('ATT_LVL', '9'))
ROPE_ADD_ENG = os.environ.get('ROPE_ADD_ENG', 'pool')
ATT_SUB = int(os.environ.get('ATT_SUB', '9'))
RES_ENG = os.environ.get('RES_ENG', 'dve')
MIX_SUB = int(os.environ.get('MIX_SUB', '9'))
O_GQ, O_GK, O_GV, O_GO, O_AL, O_SB, O_SC, O_SX, O_DQ, O_DK, O_DV = (
    0, 256, 512, 1024, 1536, 1552, 2064, 2576, 3088, 4624, 6160)

W_ATT = 0
W_GLA = W_ATT + 12 * 3072
W_ALOW = W_GLA + 4 * 3072
W_SC = W_ALOW + 256
W_MRG = W_SC + 4 * 3072
W_MIX = W_MRG + 24 * 1536
W_FF = W_MIX + 8192
W_DN = W_FF + NFF * 2048
W_TOT = W_DN + 8 * 2816

V_PRE_MIX, V_POST_MIX, V_PRE_FFN, V_POST_FFN = 0, 8, 16, 24
V_BGATE = 32
V_FFCW = 56
V_FFCB = 122
V_SCCW = 144
V_GLAG = 156
NV = 160

C_ID, C_TRIL, C_UPP, C_MASK, C_RMAT, C_INV, C_MBIAS = 0, 128, 256, 384, 640, 768, 772
NCONST = 772 + 256


class Buf:
    __slots__ = ("w", "rs", "rd")

    def __init__(self):
        self.w = None
        self.rs = {}
        self.rd = []


class Op:
    __slots__ = ("eng", "idx", "fn", "deps", "waited", "seq", "dma", "dsem", "dval", "needed")


class Prog:
    def __init__(self):
        self.ops = {e: [] for e in ENGS}
        self.dma_cnt = {}
        self.dma_last = {}
        self.dma_rr = {e: 0 for e in ENGS}
        self.last_real = {}

    def add(self, eng, fn, reads=(), writes=(), dma=False):
        op = Op()
        op.eng = eng
        op.idx = len(self.ops[eng])
        op.fn = fn
        op.dma = dma
        op.waited = False
        op.seq = 0
        op.needed = []
        deps = []
        for b in reads:
            if b.w is not None:
                deps.append((b.w, "raw"))
        for b in writes:
            if b.w is not None:
                deps.append((b.w, "waw"))
            for r in b.rs.values():
                deps.append((r, "war"))
            for r in b.rd:
                deps.append((r, "war"))
        if dma:
            k = self.dma_rr[eng] % N_DMA_SEMS
            self.dma_rr[eng] += 1
            key = (eng, k)
            prev = self.dma_last.get(key)
            if prev is not None:
                deps.append((prev, "sem"))
            self.dma_cnt[key] = self.dma_cnt.get(key, 0) + 1
            op.dsem = key
            op.dval = 16 * self.dma_cnt[key]
            self.dma_last[key] = op
        for b in reads:
            if dma:
                b.rd.append(op)
            else:
                b.rs[eng] = op
        for b in writes:
            b.w = op
            b.rs = {}
            b.rd = []
        op.deps = deps
        self.ops[eng].append(op)
        if fn is not None and not dma:
            self.last_real[eng] = op
        return op

    def barrier(self):
        lasts = list(self.last_real.values()) + list(self.dma_last.values())
        for e in ENGS:
            op = self.add(e, None)
            op.deps = [(p, "raw") for p in lasts]

    def finalize(self):
        for eng in ENGS:
            known = {}
            for op in self.ops[eng]:
                need = {}
                for (p, kind) in op.deps:
                    if p.dma:
                        key = ("dma",) + p.dsem
                        val = p.dval
                    else:
                        if p.eng == eng and eng == "pe":
                            continue
                        key = p.eng
                        val = p.idx + 1
                    if known.get(key, 0) >= val:
                        continue
                    if key not in need or need[key][0] < val:
                        need[key] = (val, p)
                for key, (val, p) in need.items():
                    known[key] = val
                    p.waited = True
                    op.needed.append(p)
        for eng in ENGS:
            n = 0
            for op in self.ops[eng]:
                if (not op.dma) and op.waited:
                    n += 1
                    op.seq = n

    def emit(self, nc, stack):
        self.finalize()
        esem = {e: stack.enter_context(nc.semaphore("s_" + e)) for e in ENGS}
        dsem = {}
        for key in self.dma_cnt:
            dsem[key] = stack.enter_context(nc.semaphore("d_%s%d" % key))
        block = stack.enter_context(nc.Block())

        def run(eng_name):
            def body(e):
                for op in self.ops[eng_name]:
                    for p in op.needed:
                        if p.dma:
                            e.wait_ge(dsem[p.dsem], p.dval)
                        else:
                            e.wait_ge(esem[p.eng], p.seq)
                    if op.fn is None:
                        continue
                    ins = op.fn(e)
                    if op.dma:
                        ins.then_inc(dsem[op.dsem], 16)
                    elif op.waited:
                        ins.then_inc(esem[eng_name], 1)
            return body

        block.tensor(run("pe"))
        block.scalar(run("act"))
        block.vector(run("dve"))
        block.gpsimd(run("pool"))
        block.sync(run("sp"))
        return {e: (len(self.ops[e]), sum(len(o.needed) for o in self.ops[e])) for e in ENGS}


class T:
    __slots__ = ("ap", "bufs")

    def __init__(self, ap, bufs):
        self.ap = ap
        self.bufs = list(bufs) if isinstance(bufs, (list, tuple)) else [bufs]


def build(nseq=2, nlayers=2, taps=None, stop=99):
    taps = taps or {}
    nc = bass.Bass("TRN2", target_bir_lowering=False)
    x_d = nc.dram_tensor("x", [nseq, S, D], F32, kind="ExternalInput").ap()
    pos_d = nc.dram_tensor("posb", [nseq, 128, S], I32, kind="ExternalInput").ap()
    wpk_d = nc.dram_tensor("wpk", [2, 128, W_TOT], F32, kind="ExternalInput").ap()
    vec_d = nc.dram_tensor("vecs", [2, 128, NV], F32, kind="ExternalInput").ap()
    wup_d = nc.dram_tensor("wup", [2, 32, 512], F32, kind="ExternalInput").ap()
    cst_d = nc.dram_tensor("consts", [128, NCONST], F32, kind="ExternalInput").ap()
    out_d = nc.dram_tensor("out", [nseq, S, D], F32, kind="ExternalOutput").ap()
    tap_d = {}
    for name, shape in taps.items():
        tap_d[name] = nc.dram_tensor("tap_" + name, list(shape), F32, kind="ExternalOutput").ap()

    P = Prog()
    st = ExitStack()
    skip = [False]
    _orig_add = P.add

    def _add(eng, fn, reads=(), writes=(), dma=False):
        if skip[0]:
            return Op()
        return _orig_add(eng, fn, reads, writes, dma)
    P.add = _add

    def phase(k):
        skip[0] = (k > stop)
        PHASE_LOG.append((k, len(P.ops["pe"])))

    def sb(name, shape, dt):
        return st.enter_context(nc.sbuf_tensor(name, list(shape), dt))

    xT = sb("xT", [128, 8, S], F32)
    xT_b = [[Buf() for _ in range(NT)] for _ in range(8)]
    cosT = sb("cosT", [128, S], BF16)
    sinT = sb("sinT", [128, S], BF16)
    tab_b = Buf()
    ident = sb("ident", [128, 128], F32)
    trilp = sb("trilp", [128, 128], F32)
    upp = sb("upp", [128, 128], F32)
    inv2 = sb("inv2", [128, 1], F32)
    maskA = sb("maskA", [128, 256], BF16)
    rmat = sb("rmat", [128, 128], BF16)
    ident_bf = sb("ident_bf", [128, 128], BF16)
    mbias = sb("mbias", [128, 256], BF16)
    ones_bf = sb("ones_bf", [128, 128], BF16)
    ones32 = sb("ones32", [32, 128], F32)
    epsT = sb("epsT", [128, 1], F32)
    vecs = sb("vecs_sb", [128, 2, NV], F32)
    cst_b = Buf()
    wst = [sb("wst%d" % i, [128, 3072], BF16) for i in range(2)]
    wst_b = [Buf(), Buf()]
    wst_rr = [0]
    RN = 59648
    posi_t = sb("posi", [128, 512], I32)
    posi_b = Buf()
    R = sb("R", [128, RN], BF16)
    print("sbuf bytes remaining", nc.sbuf_bytes_remaining)

    ps = [st.enter_context(nc.psum_tensor("ps%d" % i, [128, 512], F32)) for i in range(8)]
    ps_b = [Buf() for _ in range(8)]
    pool_banks = {"a": list(range(8)), "b": []}
    ps_rr = {"a": 0, "b": 0}

    def set_pool(banks, which="a"):
        pool_banks[which] = list(banks)
        ps_rr[which] = 0

    def bank(which="a"):
        k = pool_banks[which][ps_rr[which] % len(pool_banks[which])]
        ps_rr[which] += 1
        return T(ps[k][:, :], ps_b[k])

    def bufs_of(*ts):
        out = []
        for t in ts:
            if t is None or not isinstance(t, T):
                continue
            out.extend(t.bufs)
        return out

    def apof(t):
        return t.ap if isinstance(t, T) else t

    def sub(t, ap):
        return T(ap, t.bufs)

    def mm(out, lhsT, rhs, start=True, stop=True):
        o, l, r = out.ap, lhsT.ap, rhs.ap
        P.add("pe", lambda e: e.matmul(o, lhsT=l, rhs=r, start=start, stop=stop),
              reads=bufs_of(lhsT, rhs), writes=bufs_of(out))

    def tr(out, in_):
        o, i = out.ap, in_.ap
        P.add("pe", lambda e: e.transpose(o, i, ident[:, :]), reads=bufs_of(in_) + [cst_b], writes=bufs_of(out))

    def act(out, in_, func, scale=1.0, bias=None, extra_reads=()):
        o, i = out.ap, in_.ap
        kw = {}
        if bias is not None:
            kw["bias"] = apof(bias)
        P.add("act", lambda e: e.activation(out=o, in_=i, func=func, scale=scale, **kw),
              reads=bufs_of(in_, bias) + list(extra_reads), writes=bufs_of(out))

    def cp(eng, out, in_):
        o, i = out.ap, in_.ap
        if eng == "act":
            P.add("act", lambda e: e.copy(out=o, in_=i), reads=bufs_of(in_), writes=bufs_of(out))
        else:
            P.add(eng, lambda e: e.tensor_copy(out=o, in_=i), reads=bufs_of(in_), writes=bufs_of(out))

    def tt(eng, out, in0, in1, op):
        o, a, b = out.ap, in0.ap, in1.ap
        P.add(eng, lambda e: e.tensor_tensor(out=o, in0=a, in1=b, op=op),
              reads=bufs_of(in0, in1), writes=bufs_of(out))

    def ts(eng, out, in0, s1, s2, op0, op1=None):
        o, a = out.ap, in0.ap
        a1, a2 = apof(s1), apof(s2)
        if op1 is None:
            P.add(eng, lambda e: e.tensor_scalar(out=o, in0=a, scalar1=a1, scalar2=None, op0=op0),
                  reads=bufs_of(in0, s1), writes=bufs_of(out))
        else:
            P.add(eng, lambda e: e.tensor_scalar(out=o, in0=a, scalar1=a1, scalar2=a2, op0=op0, op1=op1),
                  reads=bufs_of(in0, s1, s2), writes=bufs_of(out))

    def stt(eng, out, in0, scalar, in1, op0, op1):
        o, a, b = out.ap, in0.ap, in1.ap
        sc = apof(scalar)
        P.add(eng, lambda e: e.scalar_tensor_tensor(out=o, in0=a, scalar=sc, in1=b, op0=op0, op1=op1),
              reads=bufs_of(in0, scalar, in1), writes=bufs_of(out))

    def memset(eng, out, val):
        o = out.ap
        P.add(eng, lambda e: e.memset(o, val), writes=bufs_of(out))

    def recip(out, in_):
        o, i = out.ap, in_.ap
        P.add("dve", lambda e: e.reciprocal(out=o, in_=i), reads=bufs_of(in_), writes=bufs_of(out))

    def dma(eng, out, in_, reads=(), writes=(), slow=False):
        if slow:
            P.add(eng, lambda e: e.dma_start(out=out, in_=in_, allow_slow_non_contiguous=True),
                  reads=list(reads), writes=list(writes), dma=True)
        else:
            P.add(eng, lambda e: e.dma_start(out=out, in_=in_), reads=list(reads), writes=list(writes), dma=True)

    def tap(name, src):
        if name in tap_d:
            d = tap_d[name]
            dma("pool", d, src.ap, reads=src.bufs)

    def V(l, off, n=1):
        return T(vecs[:, l, off:off + n], cst_b)

    class Region:
        def __init__(self):
            self.off = 0

        def reset(self, off=0):
            self.off = off

        def bf(self, n):
            a = self.off
            self.off += n
            assert self.off <= RN, ("region overflow", self.off, RN)
            return R[:, a:a + n]

        def f32(self, n):
            if self.off % 2:
                self.off += 1
            a = self.off
            self.off += 2 * n
            assert self.off <= RN, ("region overflow", self.off, RN)
            return R[:, a:a + 2 * n].bitcast(F32)

        def i32(self, n):
            if self.off % 2:
                self.off += 1
            a = self.off
            self.off += 2 * n
            assert self.off <= RN
            return R[:, a:a + 2 * n].bitcast(I32)

    reg = Region()

    class Ring:
        def __init__(self, aps):
            self.items = [T(a, Buf()) for a in aps]
            self.i = 0

        def next(self):
            t = self.items[self.i % len(self.items)]
            self.i += 1
            return t

    pend = [None]
    WSPLIT = 1024
    wst_pb = [[Buf(), Buf()], [Buf(), Buf()]]

    def _mk_dma(k):
        cells = []
        for part in range(2):
            cell = {}
            P.add("pool", (lambda cell: (lambda e: e.dma_start(out=cell["out"], in_=cell["in"])))(cell),
                  writes=[wst_pb[k][part]], dma=True)
            cells.append(cell)
        return cells

    def _fill(cells, k, l, off, n):
        a = min(n, WSPLIT)
        cells[0]["out"] = wst[k][:, 0:a]
        cells[0]["in"] = wpk_d[l, :, off:off + a]
        if n > WSPLIT:
            cells[1]["out"] = wst[k][:, WSPLIT:n]
            cells[1]["in"] = wpk_d[l, :, off + WSPLIT:off + n]
        else:
            cells[1]["out"] = wst[k][:, WSPLIT:WSPLIT + 16]
            cells[1]["in"] = wpk_d[l, :, off:off + 16]

    def wload(l, off, n):
        k = wst_rr[0] % 2
        wst_rr[0] += 1
        if skip[0]:
            return k
        if pend[0] is None:
            cells = _mk_dma(k)
        else:
            cells = pend[0]
        _fill(cells, k, l, off, n)
        pend[0] = _mk_dma((k + 1) % 2)
        return k

    def wflush():
        if pend[0] is not None:
            k = wst_rr[0] % 2
            _fill(pend[0], k, 0, 0, 16)
            pend[0] = None

    def wblk(k, off, kc, w):
        end = off + kc * w
        if end <= WSPLIT:
            bufs = [wst_pb[k][0]]
        elif off >= WSPLIT:
            bufs = [wst_pb[k][1]]
        else:
            bufs = list(wst_pb[k])
        return T(wst[k][:, off:end].rearrange("p (c w) -> p c w", c=kc), bufs)

    def perm(ap, d, n0, w):
        L = S // d
        if d == 1:
            return ap[:, n0:n0 + w], 1
        v = ap.rearrange("p (l r) -> p r l", r=d)
        if w <= L:
            r = n0 // L
            l0 = n0 % L
            return v[:, r, l0:l0 + w], 1
        k = w // L
        r0 = n0 // L
        return v[:, r0:r0 + k, :], k

    def permw(ap, d, i):
        if d == 1:
            return ap[:, i * 512:(i + 1) * 512]
        n = 512 // d
        return ap.rearrange("p (r l) -> p l r", r=d)[:, i * n:(i + 1) * n, :]

    def shp2(ap, d):
        if d == 1:
            return ap
        return ap.rearrange("p (l r) -> p l r", r=d)

    def shp(ap, k):
        if k == 1:
            return ap
        return ap.rearrange("p (k l) -> p k l", k=k)

    dma("sp", ident[:, :], cst_d[:, C_ID:C_ID + 128], writes=[cst_b])
    dma("sp", trilp[:, :], cst_d[:, C_TRIL:C_TRIL + 128], writes=[cst_b])
    dma("sp", upp[:, :], cst_d[:, C_UPP:C_UPP + 128], writes=[cst_b])
    dma("sp", inv2[:, :], cst_d[:, C_INV:C_INV + 1], writes=[cst_b], slow=True)
    dma("pool", maskA[:, :], cst_d[:, C_MASK:C_MASK + 256], writes=[cst_b])
    dma("pool", rmat[:, :], cst_d[:, C_RMAT:C_RMAT + 128], writes=[cst_b])
    dma("pool", ident_bf[:, :], cst_d[:, C_ID:C_ID + 128], writes=[cst_b])
    dma("pool", mbias[:, :], cst_d[:, C_MBIAS:C_MBIAS + 256], writes=[cst_b])
    dma("sp", vecs[:, :, :], vec_d.rearrange("l p n -> p l n"), writes=[cst_b])
    memset("dve", T(ones_bf[:, :], cst_b), 1.0)
    memset("dve", T(ones32[:, :], cst_b), 1.0)
    memset("dve", T(epsT[:, :], cst_b), EPS)
    P.barrier()
    ONES = T(ones_bf[:, :], cst_b)
    MASKA = T(maskA[:, :], cst_b)
    MASKC = T(maskA[:, 0:128], cst_b)
    RMAT = T(rmat[:, :], cst_b)
    EPSB = T(epsT[:, :], cst_b)

    def xt(c, i, w=512):
        return T(xT[:, c, i * 512:i * 512 + w], xT_b[c][i])

    def rstd_from_ss(ss_ps, n_feat, tmp_ring, parts=128):
        t = tmp_ring.next()
        tp = sub(t, t.ap[0:parts, :])
        act(tp, sub(ss_ps, ss_ps.ap[0:parts, :]), AF.Ln, scale=1.0 / n_feat, bias=sub(EPSB, epsT[0:parts, :]))
        act(tp, tp, AF.Exp, scale=-0.5)
        return tp

    def rmsnorm_h(l, goff, hT, hT_b, tiles, f32ring, sqring):
        for k, i in enumerate(tiles):
            ss = bank()
            for c in range(8):
                sq = sqring.next()
                act(sq, xt(c, i), AF.Square)
                mm(ss, ONES, sq, start=(c == 0), stop=(c == 7))
            rs = rstd_from_ss(ss, D, f32ring)
            for c in range(8):
                stt("dve", T(hT[:, c, k * 512:(k + 1) * 512], hT_b[c][k]), xt(c, i), V(l, goff + c), rs, ALU.mult, ALU.mult)

    for sq_i in range(nseq):
        phase(0)
        P.barrier()
        reg.reset()
        set_pool(range(8))
        xin = Ring([reg.f32(1024) for _ in range(3)])
        for tt_i in range(16):
            xi = xin.next()
            dma("sp", xi.ap, x_d[sq_i, tt_i * 128:(tt_i + 1) * 128, :], writes=xi.bufs)
            for half in range(2):
                pb = bank()
                for cc in range(4):
                    c = half * 4 + cc
                    tr(sub(pb, pb.ap[:, cc * 128:(cc + 1) * 128]), sub(xi, xi.ap[:, c * 128:(c + 1) * 128]))
                dst = T(xT[:, half * 4:half * 4 + 4, tt_i * 128:(tt_i + 1) * 128],
                        [xT_b[half * 4 + cc][tt_i // 4] for cc in range(4)])
                cp("act" if half == 0 else "dve", dst, sub(pb, pb.ap.rearrange("p (c t) -> p c t", c=4)))
        phase(1)
        P.barrier()
        reg.reset()
        tA = T(reg.f32(S), Buf())
        tB = T(reg.f32(S), Buf())
        tC = T(reg.f32(S), Buf())
        for pi_ in range(4):
            dma("sp", posi_t[:, :], pos_d[sq_i, :, pi_ * 512:(pi_ + 1) * 512], writes=[posi_b])
            cp("pool", sub(tA, tA.ap[:, pi_ * 512:(pi_ + 1) * 512]), T(posi_t[:, :], posi_b))
        ts("dve", tA, tA, T(inv2[:, 0:1], cst_b), None, ALU.mult)
        MG = 12582912.0
        TAB = T(sinT[:, :], tab_b)
        TABC = T(cosT[:, :], tab_b)
        for which, shift, dst in ((0, 0.0, TAB), (1, 0.25, TABC)):
            ts("dve", tB, tA, 1.0 / (2 * math.pi), shift, ALU.mult, ALU.add)
            ts("dve", tC, tB, MG, -MG, ALU.add, ALU.add)
            tt("dve", tB, tB, tC, ALU.subtract)
            act(dst, tB, AF.Sin, scale=6.283185)

        for l in range(nlayers):
            phase(2)
            P.barrier()
            reg.reset()
            set_pool(range(8))
            hT = reg.bf(8 * S).rearrange("p (c t) -> p c t", c=8)
            hT_b = [[Buf() for _ in range(NT)] for _ in range(8)]
            mg_off = reg.off
            mg = reg.bf(8 * S).rearrange("p (c t) -> p c t", c=8)
            mg_b = [[Buf() for _ in range(NT)] for _ in range(8)]
            yT_off = reg.off
            yT = reg.bf(4 * S).rearrange("p (c t) -> p c t", c=4)
            yT_b = [[Buf() for _ in range(NT)] for _ in range(4)]
            base_off = reg.off
            f32ring = Ring([reg.f32(512) for _ in range(3)])
            sqring = Ring([reg.bf(512) for _ in range(3)])
            rmsnorm_h(l, V_PRE_MIX, hT, hT_b, list(range(NT)), f32ring, sqring)
            if sq_i == 0 and l == 0:
                tap("hT0", T(hT[:, 0, :], [hT_b[0][i] for i in range(NT)]))

            def hTt(c, i):
                return T(hT[:, c, i * 512:(i + 1) * 512], hT_b[c][i])

            def hT_all(c):
                return [hT_b[c][i] for i in range(NT)]

            def merge_pass(g, first):
                set_pool(range(8))
                for j in range(8):
                    k = wload(l, W_MRG + (g * 8 + j) * 1536, 1536)
                    wg = wblk(k, 0, 8, 128)
                    wb = wblk(k, 1024, 4, 128)
                    for i in range(NT):
                        gp = bank()
                        for c in range(8):
                            mm(gp, sub(wg, wg.ap[:, c, :]), hTt(c, i), start=(c == 0), stop=(c == 7))
                        pp = bank()
                        for c in range(4):
                            mm(pp, sub(wb, wb.ap[:, c, :]), T(yT[:, c, i * 512:(i + 1) * 512], yT_b[c][i]),
                               start=(c == 0), stop=(c == 3))
                        sg = mtmp.next()
                        act(sg, gp, AF.Sigmoid, bias=V(l, V_BGATE + g * 8 + j))
                        mt = T(mg[:, j, i * 512:(i + 1) * 512], mg_b[j][i])
                        if first:
                            tt("dve", mt, sg, pp, ALU.mult)
                        else:
                            sgb = T(sg.ap.bitcast(BF16)[:, 0:512], sg.bufs)
                            tt("dve", sgb, sg, pp, ALU.mult)
                            tt("pool", mt, mt, sgb, ALU.add)

            phase(3)
            P.barrier()
            reg.reset(base_off)
            oacc = reg.f32(S)
            dacc = reg.f32(S)
            oacc_b = Buf()
            dacc_b = Buf()
            pring = Ring([reg.bf(256) for _ in range(4)])
            qsb = Ring([reg.bf(512) for _ in range(2)])
            usb = Ring([reg.bf(512) for _ in range(2)])
            t1r = Ring([reg.f32(512) for _ in range(3)])
            mtmp = t1r
            att_end = reg.off
            reg.reset(mg_off)
            qkv = []
            for _ in range(2):
                qkv.append(dict(q=reg.bf(S), k=reg.bf(S), v=reg.bf(S),
                                qb=[Buf() for _ in range(NT)], kb=[Buf() for _ in range(NT)],
                                vb=[Buf() for _ in range(NT)]))
            vts = reg.bf(S)
            vts_b = [Buf() for _ in range(NT)]
            assert reg.off <= yT_off
            reg.reset(att_end)
            scale = 128.0 ** -0.5
            units = [(h, g) for h in range(4) for g in GSET]
            set_pool([0, 1, 2], "b")
            set_pool([3, 4, 5], "a")
            odbank = [T(ps[6][:, :], ps_b[6]), T(ps[7][:, :], ps_b[7])]

            def proj_gen(u):
                h, g = units[u]
                bset = qkv[u % 2]
                d = DILS[g]
                k = wload(l, W_ATT + (g * 4 + h) * 3072, 3072)
                wq = wblk(k, 0, 8, 128)
                wk = wblk(k, 1024, 8, 128)
                wv = wblk(k, 2048, 8, 128)
                for (w_, dst, dst_b) in ((wq, bset["q"], bset["qb"]), (wk, bset["k"], bset["kb"])):
                    pendR = None
                    for i in range(NT + 1):
                        if i < NT:
                            pp = bank("a")
                            for c in range(8):
                                mm(pp, sub(w_, w_.ap[:, c, :]), hTt(c, i), start=(c == 0), stop=(c == 7))
                            qs = qsb.next()
                            cp("act", qs, pp)
                            t1 = t1r.next()
                            tt("pool", t1, qs, T(cosT[:, i * 512:(i + 1) * 512], tab_b), ALU.mult)
                            us = usb.next()
                            tt("pool", us, qs, T(sinT[:, i * 512:(i + 1) * 512], tab_b), ALU.mult)
                        if pendR is not None:
                            (pus, pt1, pi) = pendR
                            rp = bank("a")
                            mm(rp, RMAT, pus)
                            wb_ = [dst_b[pi]] if d == 1 else list(dst_b)
                            tt("dve", T(permw(dst, d, pi), wb_), sub(pt1, shp2(pt1.ap, d)), sub(rp, shp2(rp.ap, d)), ALU.add)
                        pendR = (us, t1, i) if i < NT else None
                        yield
                for i in range(NT):
                    pp = bank("a")
                    for c in range(8):
                        mm(pp, sub(wv, wv.ap[:, c, :]), hTt(c, i), start=(c == 0), stop=(c == 7))
                    wb_ = [vts_b[i]] if d == 1 else list(vts_b)
                    cp("act", T(permw(vts, d, i), wb_), sub(pp, shp2(pp.ap, d)))
                    yield
                for i in range(NT):
                    pp = bank("a")
                    for bb in range(4):
                        B = i * 4 + bb
                        mm(sub(pp, pp.ap[:, bb * 128:(bb + 1) * 128]), T(vts[:, B * 128:(B + 1) * 128], vts_b[i]),
                           T(ident_bf[:, :], cst_b))
                    cp("act", T(bset["v"][:, i * 512:(i + 1) * 512], bset["vb"][i]), pp)
                    yield

            def core_gen(u):
                h, g = units[u]
                bset = qkv[u % 2]
                qT, kT, vtk = bset["q"], bset["k"], bset["v"]
                qT_b, kT_b, v_b = bset["qb"], bset["kb"], bset["vb"]
                d = DILS[g]
                L = S // d
                nb = L // 128
                first = (g == GSET[0])
                last = (g == GSET[-1])
                pt = {}

                def scores(B):
                    n = B % nb
                    wq_ = 256 if n + 1 < nb else 128
                    sp_ = bank("b")
                    rb = [qT_b[(B * 128) // 512]]
                    if wq_ == 256 and ((B + 1) * 128) // 512 != (B * 128) // 512:
                        rb.append(qT_b[((B + 1) * 128) // 512])
                    mm(sub(sp_, sp_.ap[:, 0:wq_]), T(kT[:, B * 128:(B + 1) * 128], kT_b[B // 4]),
                       T(qT[:, B * 128:B * 128 + wq_], rb), start=True, stop=False)
                    mm(sub(sp_, sp_.ap[:, 0:wq_]), T(ident_bf[:, :], cst_b), T(mbias[:, 0:wq_], cst_b), start=False, stop=True)
                    p_ = pring.next()
                    act(sub(p_, p_.ap[:, 0:wq_]), sub(sp_, sp_.ap[:, 0:wq_]), AF.Exp, scale=scale)
                    pt[B] = p_

                scores(0)
                yield
                for B in range(16):
                    if B + 1 < 16:
                        scores(B + 1)
                    n = B % nb
                    i2 = B // 2
                    ob = odbank[i2 % 2]
                    osl = sub(ob, ob.ap[:, (B % 2) * 128:(B % 2 + 1) * 128])
                    dsl = sub(ob, ob.ap[:, 256 + (B % 2) * 128:256 + (B % 2 + 1) * 128])
                    vcur = T(vtk[:, B * 128:(B + 1) * 128], v_b[B // 4])
                    pc = sub(pt[B], pt[B].ap[:, 0:128])
                    if n > 0:
                        vprev = T(vtk[:, (B - 1) * 128:B * 128], v_b[(B - 1) // 4])
                        ppv = sub(pt[B - 1], pt[B - 1].ap[:, 128:256])
                        mm(osl, vprev, ppv, start=True, stop=False)
                        mm(osl, vcur, pc, start=False, stop=True)
                        mm(dsl, ONES, ppv, start=True, stop=False)
                        mm(dsl, ONES, pc, start=False, stop=True)
                    else:
                        mm(osl, vcur, pc, start=True, stop=True)
                        mm(dsl, ONES, pc, start=True, stop=True)
                    if B % 2 == 1:
                        ov, kk_ = perm(oacc, d, i2 * 256, 256)
                        dv, _ = perm(dacc, d, i2 * 256, 256)
                        OA = T(ov, oacc_b)
                        DA = T(dv, dacc_b)
                        o_src = sub(ob, shp(ob.ap[:, 0:256], kk_))
                        d_src = sub(ob, shp(ob.ap[:, 256:512], kk_))
                        if first:
                            cp("act", OA, o_src)
                            cp("act", DA, d_src)
                        else:
                            tt("dve", OA, OA, o_src, ALU.add)
                            tt("dve", DA, DA, d_src, ALU.add)
                    yield
                if last:
                    DAF = T(dacc, dacc_b)
                    recip(DAF, DAF)
                    tt("dve", T(yT[:, h, :], [yT_b[h][i] for i in range(NT)]), T(oacc, oacc_b), DAF, ALU.mult)
                    yield

            def drive(ga, gb):
                ga = iter(ga) if ga is not None else None
                gb = iter(gb) if gb is not None else None
                while ga is not None or gb is not None:
                    if ga is not None:
                        try:
                            next(ga)
                        except StopIteration:
                            ga = None
                    if gb is not None:
                        try:
                            next(gb)
                        except StopIteration:
                            gb = None

            phase(3)
            drive(proj_gen(0), None)
            for u in range(len(units)):
                drive(proj_gen(u + 1) if u + 1 < len(units) else None, core_gen(u))
            set_pool(range(8), "a")
            if sq_i == 0 and l == 0:
                tap("ydil", T(yT[:, 0, :], [yT_b[0][i] for i in range(NT)]))
            phase(4)
            merge_pass(2, True)

            phase(5)
            P.barrier()
            reg.reset(base_off)
            set_pool(range(8))
            l1 = reg.f32(16 * 256).rearrange("p (n f) -> p n f", n=16)
            l1_b = [Buf() for _ in range(16)]
            gla_off = reg.off
            alow = reg.f32(S)
            alow_b = [Buf() for _ in range(NT)]
            e1r = Ring([reg.f32(512) for _ in range(2)])
            wups = reg.f32(512)
            wups_b = Buf()
            dma("sp", wups[0:32, :], wup_d[l, :, :], writes=[wups_b])
            k = wload(l, W_ALOW, 256)
            wa = wblk(k, 0, 8, 32)
            for i in range(NT):
                pp = bank()
                for c in range(8):
                    mm(sub(pp, pp.ap[0:32, :]), sub(wa, wa.ap[:, c, :]), hTt(c, i), start=(c == 0), stop=(c == 7))
                cp("act", T(alow[0:32, i * 512:(i + 1) * 512], alow_b[i]), sub(pp, pp.ap[0:32, :]))
            WUP = T(wups[0:32, 0:256], wups_b)
            BVEC = T(wups[0:32, 256:512], wups_b)
            O32 = T(ones32[0:32, :], cst_b)
            for n2 in range(8):
                pp = bank()
                for u in range(2):
                    n = n2 * 2 + u
                    o_ = sub(pp, pp.ap[:, u * 256:(u + 1) * 256])
                    mm(o_, T(alow[0:32, n * 128:(n + 1) * 128], alow_b[n // 4]), WUP, start=True, stop=False)
                    mm(o_, O32, BVEC, start=False, stop=True)
                e1 = e1r.next()
                act(e1, pp, AF.Exp, scale=-1.0)
                act(T(l1[:, n2 * 2:n2 * 2 + 2, :], [l1_b[n2 * 2], l1_b[n2 * 2 + 1]]),
                    sub(e1, e1.ap.rearrange("p (n f) -> p n f", n=2)), AF.Ln, bias=1.0)
            if sq_i == 0 and l == 0:
                tap("l1", T(l1[:, 0, :], l1_b[0]))
            P.barrier()
            reg.reset(gla_off)
            phase(6)
            qqr = Ring([reg.bf(512) for _ in range(2)])
            kkr = Ring([reg.bf(512) for _ in range(2)])
            vhr = Ring([reg.bf(512) for _ in range(2)])
            kinr = Ring([reg.bf(256) for _ in range(2)])
            gsr = Ring([reg.f32(512) for _ in range(1)])
            eLr = Ring([reg.f32(512) for _ in range(1)])
            enLr = Ring([reg.f32(512) for _ in range(1)])
            eRr = Ring([reg.f32(256) for _ in range(1)])
            dec = T(reg.f32(16), Buf())
            Sst = T(reg.f32(128), Buf())
            Sbf = T(reg.bf(128), Buf())
            pmr = Ring([reg.bf(128) for _ in range(2)])
            osq = Ring([reg.bf(512) for _ in range(1)])
            y1r = Ring([reg.f32(512) for _ in range(2)])
            mtmp = y1r
            TRIL = T(trilp[:, :], cst_b)
            UPP = T(upp[:, :], cst_b)
            obank = [T(ps[6][:, :], ps_b[6]), T(ps[7][:, :], ps_b[7])]
            set_pool(range(6))
            for h in range(4):
                k = wload(l, W_GLA + h * 3072, 3072)
                wh = wblk(k, 0, 8, 384)
                memset("dve", sub(Sst, Sst.ap[0:64, :]), 0.0)
                for i in range(NT):
                    lt = bank()
                    for u in range(4):
                        n = i * 4 + u
                        mm(sub(lt, lt.ap[0:64, u * 128:(u + 1) * 128]), T(l1[:, n, h * 64:(h + 1) * 64], l1_b[n]), TRIL)
                    eL = eLr.next()
                    enL = enLr.next()
                    act(sub(eL, eL.ap[0:64, :]), sub(lt, lt.ap[0:64, :]), AF.Exp)
                    act(sub(enL, enL.ap[0:64, :]), sub(lt, lt.ap[0:64, :]), AF.Exp, scale=-1.0)
                    cp("pool", sub(dec, dec.ap[0:64, i * 4:i * 4 + 4]),
                       sub(eL, eL.ap[0:64, :].rearrange("p (u t) -> p u t", u=4)[:, :, 127]))
                    pq = bank()
                    for c in range(8):
                        mm(sub(pq, pq.ap[0:64, :]), sub(wh, wh.ap[:, c, 0:64]), hTt(c, i), start=(c == 0), stop=(c == 7))
                    qq = qqr.next()
                    stt("dve", sub(qq, qq.ap[0:64, :]), sub(pq, pq.ap[0:64, :]), 0.125,
                        sub(eL, eL.ap[0:64, :]), ALU.mult, ALU.mult)
                    pk = bank()
                    for c in range(8):
                        mm(sub(pk, pk.ap[0:64, :]), sub(wh, wh.ap[:, c, 64:128]), hTt(c, i), start=(c == 0), stop=(c == 7))
                    kk = kkr.next()
                    tt("dve", sub(kk, kk.ap[0:64, :]), sub(pk, pk.ap[0:64, :]), sub(enL, enL.ap[0:64, :]), ALU.mult)
                    pg = bank()
                    for c in range(8):
                        mm(pg, sub(wh, wh.ap[:, c, 256:384]), hTt(c, i), start=(c == 0), stop=(c == 7))
                    gs = gsr.next()
                    act(gs, pg, AF.Silu)
                    lr = bank()
                    pkt = bank()
                    for u in range(4):
                        n = i * 4 + u
                        mm(sub(lr, lr.ap[:, u * 64:(u + 1) * 64]), UPP, T(l1[:, n, h * 64:(h + 1) * 64], l1_b[n]))
                        for c in range(8):
                            hv = T(hT[:, c, n * 128:(n + 1) * 128], hT_b[c][i])
                            mm(sub(pkt, pkt.ap[:, u * 64:(u + 1) * 64]), hv, sub(wh, wh.ap[:, c, 64:128]),
                               start=(c == 0), stop=(c == 7))
                    eR = eRr.next()
                    act(eR, sub(lr, lr.ap[:, 0:256]), AF.Exp)
                    kin = kinr.next()
                    tt("dve", kin, sub(pkt, pkt.ap[:, 0:256]), eR, ALU.mult)
                    pvt = bank()
                    for u in range(4):
                        n = i * 4 + u
                        for c in range(8):
                            hv = T(hT[:, c, n * 128:(n + 1) * 128], hT_b[c][i])
                            mm(sub(pvt, pvt.ap[:, u * 128:(u + 1) * 128]), hv, sub(wh, wh.ap[:, c, 128:256]),
                               start=(c == 0), stop=(c == 7))
                    vh = vhr.next()
                    cp("act", vh, pvt)
                    ob = obank[i % 2]
                    for u in range(4):
                        n = i * 4 + u
                        qqc = sub(qq, qq.ap[0:64, u * 128:(u + 1) * 128])
                        kkc = sub(kk, kk.ap[0:64, u * 128:(u + 1) * 128])
                        vc = sub(vh, vh.ap[:, u * 128:(u + 1) * 128])
                        sp_ = bank()
                        mm(sub(sp_, sp_.ap[:, 0:128]), kkc, qqc)
                        pm = pmr.next()
                        stt("dve", pm, sub(sp_, sp_.ap[:, 0:128]), -16.0, TRIL, ALU.mult, ALU.mult)
                        osl = sub(ob, ob.ap[:, u * 128:(u + 1) * 128])
                        if n == 0:
                            mm(osl, vc, pm, start=True, stop=True)
                        else:
                            mm(osl, vc, pm, start=True, stop=False)
                            mm(osl, sub(Sbf, Sbf.ap[0:64, :]), qqc, start=False, stop=True)
                        if n < 15:
                            up_ = bank()
                            mm(sub(up_, up_.ap[0:64, 0:128]), sub(kin, kin.ap[:, u * 64:(u + 1) * 64]), vc)
                            if n == 0:
                                cp("dve", sub(Sst, Sst.ap[0:64, :]), sub(up_, up_.ap[0:64, 0:128]))
                            else:
                                stt("dve", sub(Sst, Sst.ap[0:64, :]), sub(Sst, Sst.ap[0:64, :]),
                                    sub(dec, dec.ap[0:64, n:n + 1]), sub(up_, up_.ap[0:64, 0:128]), ALU.mult, ALU.add)
                            cp("act", sub(Sbf, Sbf.ap[0:64, :]), sub(Sst, Sst.ap[0:64, :]))
                    o2 = osq.next()
                    act(o2, ob, AF.Square)
                    ss = bank()
                    mm(ss, ONES, o2)
                    rs = rstd_from_ss(ss, 128, y1r)
                    y1 = y1r.next()
                    stt("dve", y1, ob, V(l, V_GLAG), rs, ALU.mult, ALU.mult)
                    tt("dve", T(yT[:, h, i * 512:(i + 1) * 512], yT_b[h][i]), y1, gs, ALU.mult)
            if sq_i == 0 and l == 0:
                tap("ygla", T(yT[:, 0, :], [yT_b[0][i] for i in range(NT)]))
            phase(7)
            merge_pass(0, False)

            phase(8)
            P.barrier()
            reg.reset(base_off)
            set_pool(range(8))
            ub = reg.f32(S + 2)
            ub_b = Buf()
            xsr = Ring([reg.f32(512) for _ in range(2)])
            accr = Ring([reg.f32(512) for _ in range(2)])
            mtmp = Ring([reg.f32(512) for _ in range(2)])
            memset("dve", T(ub[:, 0:2], ub_b), 0.0)
            wmx_off = reg.off
            wmx = reg.bf(8192).rearrange("p (a b w) -> p a b w", a=8, b=8)
            wmx_b = Buf()
            for a in range(4):
                dma("pool", wmx[:, 2 * a:2 * a + 2, :, :].rearrange("p a b w -> p (a b w)"),
                    wpk_d[l, :, W_MIX + a * 2048:W_MIX + (a + 1) * 2048], writes=[wmx_b])
            for ch in range(4):
                k = wload(l, W_SC + ch * 3072, 3072)
                wb_ = wblk(k, 0, 8, 128)
                wc_ = wblk(k, 1024, 8, 128)
                wx_ = wblk(k, 2048, 8, 128)
                for i in range(NT):
                    pb_ = bank()
                    pc_ = bank()
                    px_ = bank()
                    for (w_, p_) in ((wb_, pb_), (wc_, pc_), (wx_, px_)):
                        for c in range(8):
                            mm(p_, sub(w_, w_.ap[:, c, :]), hTt(c, i), start=(c == 0), stop=(c == 7))
                    xs = xsr.next()
                    cp("act", xs, px_)
                    UB = T(ub[:, 2 + i * 512:2 + (i + 1) * 512], ub_b)
                    tt("dve", UB, pc_, xs, ALU.mult)
                    acc = accr.next()
                    act(acc, UB, AF.Copy, scale=vecs[:, l, V_SCCW + 2 * 4 + ch:V_SCCW + 2 * 4 + ch + 1])
                    stt("dve", acc, T(ub[:, 1 + i * 512:1 + (i + 1) * 512], ub_b), V(l, V_SCCW + 1 * 4 + ch), acc, ALU.mult, ALU.add)
                    stt("dve", acc, T(ub[:, i * 512:(i + 1) * 512], ub_b), V(l, V_SCCW + 0 * 4 + ch), acc, ALU.mult, ALU.add)
                    tt("dve", T(yT[:, ch, i * 512:(i + 1) * 512], yT_b[ch][i]), pb_, acc, ALU.mult)
            if sq_i == 0 and l == 0:
                tap("ysc", T(yT[:, 0, :], [yT_b[0][i] for i in range(NT)]))
            merge_pass(1, False)
            if sq_i == 0 and l == 0:
                tap("mg", T(mg[:, 0, :], [mg_b[0][i] for i in range(NT)]))

            phase(9)
            P.barrier()
            reg.reset(yT_off)
            set_pool(range(8))
            mo2 = [R[:, kk2 * 8192:(kk2 + 1) * 8192].bitcast(F32).rearrange("p (c t) -> p c t", c=8) for kk2 in range(2)]
            mo2_b = [[Buf() for _ in range(8)] for _ in range(2)]
            sqr = Ring([reg.bf(512) for _ in range(3)])
            rsr = Ring([reg.f32(512) for _ in range(2)])
            tmr = Ring([reg.f32(512) for _ in range(3)])
            assert reg.off <= wmx_off, (reg.off, wmx_off)
            set_pool(range(6))
            ssm = [T(ps[6][:, :], ps_b[6]), T(ps[7][:, :], ps_b[7])]
            pendE = None

            def mix_epi(pe_, jo):
                (ei, ers, emo, emo_b) = pe_
                tm = tmr.next()
                stt("dve", tm, T(emo[:, jo, :], emo_b[jo]), V(l, V_POST_MIX + jo), ers, ALU.mult, ALU.mult)
                tt("dve", xt(jo, ei), xt(jo, ei), tm, ALU.add)

            for i in range(NT):
                ss = ssm[i % 2]
                mo = mo2[i % 2]
                mo_b = mo2_b[i % 2]
                pendS = None
                for jo in range(8):
                    pp = bank()
                    for j in range(8):
                        mm(pp, T(wmx[:, jo, j, :], wmx_b), T(mg[:, j, i * 512:(i + 1) * 512], mg_b[j][i]),
                           start=(j == 0), stop=(j == 7))
                    cp("dve", T(mo[:, jo, :], mo_b[jo]), pp)
                    s2 = sqr.next()
                    act(s2, T(mo[:, jo, :], mo_b[jo]), AF.Square)
                    if pendS is not None:
                        mm(ss, ONES, pendS[0], start=(pendS[1] == 0), stop=False)
                    pendS = (s2, jo)
                    if pendE is not None:
                        mix_epi(pendE, jo)
                mm(ss, ONES, pendS[0], start=False, stop=True)
                rs = rstd_from_ss(ss, D, rsr)
                pendE = (i, rs, mo, mo_b)
            for jo in range(8):
                mix_epi(pendE, jo)
            if sq_i == 0 and l == 0:
                tap("xmid", T(xT[:, 0, :], [xT_b[0][i] for i in range(NT)]))

            phase(10)
            P.barrier()
            reg.reset()
            hal = reg.f32(NFF * 2).rearrange("p (f t) -> p f t", f=NFF)
            hal_b = Buf()
            ffbase = reg.off
            P.barrier()
            reg.reset(ffbase)
            set_pool(range(6))
            hTh = reg.bf(8 * 1024).rearrange("p (c t) -> p c t", c=8)
            hTh_b = [[Buf() for _ in range(2)] for _ in range(8)]
            hid = reg.bf(NFF * 1024).rearrange("p (f t) -> p f t", f=NFF)
            hid_b = [[Buf() for _ in range(2)] for _ in range(NFF)]
            yo = reg.f32(8 * 1024).rearrange("p (c t) -> p c t", c=8)
            yo_b = [[Buf() for _ in range(2)] for _ in range(8)]
            gbr = [(reg.f32(1026), Buf()) for _ in range(2)]
            accr = Ring([reg.f32(512) for _ in range(2)])
            ger = Ring([reg.f32(512) for _ in range(2)])
            f32ring = Ring([reg.f32(512) for _ in range(2)])
            sqring = Ring([reg.bf(512) for _ in range(3)])
            ssb = [T(ps[6][:, :], ps_b[6]), T(ps[7][:, :], ps_b[7])]

            def ffn_epilogue(half):
                for i in range(2):
                    rs = rstd_from_ss(ssb[i], D, f32ring)
                    gi = half * 2 + i
                    for j in range(8):
                        tm = ger.next()
                        stt("dve", tm, T(yo[:, j, i * 512:(i + 1) * 512], yo_b[j][i]), V(l, V_POST_FFN + j), rs, ALU.mult, ALU.mult)
                        tt("dve", xt(j, gi), xt(j, gi), tm, ALU.add)

            for half in range(2):
                rmsnorm_h(l, V_PRE_FFN, hTh, hTh_b, [half * 2, half * 2 + 1], f32ring, sqring)
                if half == 1:
                    ffn_epilogue(0)
                for f in range(NFF):
                    k = wload(l, W_FF + f * 2048, 2048)
                    wg = wblk(k, 0, 8, 128)
                    wu = wblk(k, 1024, 8, 128)
                    gb, gb_b = gbr[f % 2]
                    if half == 0:
                        memset("dve", T(gb[:, 0:2], gb_b), 0.0)
                    else:
                        cp("dve", T(gb[:, 0:2], gb_b), T(hal[:, f, :], hal_b))
                    for i in range(2):
                        pg = bank()
                        pu = bank()
                        for (w_, p_) in ((wg, pg), (wu, pu)):
                            for c in range(8):
                                mm(p_, sub(w_, w_.ap[:, c, :]), T(hTh[:, c, i * 512:(i + 1) * 512], hTh_b[c][i]),
                                   start=(c == 0), stop=(c == 7))
                        GB = T(gb[:, 2 + i * 512:2 + (i + 1) * 512], gb_b)
                        cp("act", GB, pg)
                        acc = accr.next()
                        act(acc, pg, AF.Identity, scale=vecs[:, l, V_FFCW + 2 * NFF + f:V_FFCW + 2 * NFF + f + 1],
                            bias=V(l, V_FFCB + f))
                        stt("dve", acc, T(gb[:, 1 + i * 512:1 + (i + 1) * 512], gb_b), V(l, V_FFCW + 1 * NFF + f), acc, ALU.mult, ALU.add)
                        stt("dve", acc, T(gb[:, i * 512:(i + 1) * 512], gb_b), V(l, V_FFCW + 0 * NFF + f), acc, ALU.mult, ALU.add)
                        ge = ger.next()
                        act(ge, acc, AF.Gelu_apprx_tanh)
                        tt("dve", T(hid[:, f, i * 512:(i + 1) * 512], hid_b[f][i]), ge, pu, ALU.mult)
                    if half == 0:
                        cp("dve", T(hal[:, f, :], hal_b), T(gb[:, 1024:1026], gb_b))
                pendS = None
                for j in range(8):
                    k = wload(l, W_DN + j * 2816, 2816)
                    wd = wblk(k, 0, NFF, 128)
                    for i in range(2):
                        pp = bank()
                        for f in range(NFF):
                            mm(pp, sub(wd, wd.ap[:, f, :]), T(hid[:, f, i * 512:(i + 1) * 512], hid_b[f][i]),
                               start=(f == 0), stop=(f == NFF - 1))
                        cp("dve", T(yo[:, j, i * 512:(i + 1) * 512], yo_b[j][i]), pp)
                        s2 = sqring.next()
                        act(s2, T(yo[:, j, i * 512:(i + 1) * 512], yo_b[j][i]), AF.Square)
                        if pendS is not None:
                            mm(ssb[pendS[2]], ONES, pendS[0], start=(pendS[1] == 0), stop=(pendS[1] == 7))
                        pendS = (s2, j, i)
                mm(ssb[pendS[2]], ONES, pendS[0], start=(pendS[1] == 0), stop=(pendS[1] == 7))
            ffn_epilogue(1)
            set_pool(range(8))
            if sq_i == 0 and l == 0:
                tap("xl0", T(xT[:, 0, :], [xT_b[0][i] for i in range(NT)]))

        phase(0)
        P.barrier()
        reg.reset()
        set_pool(range(8))
        oring = Ring([reg.f32(1024) for _ in range(3)])
        for tt_i in range(16):
            ot = oring.next()
            for half in range(2):
                pb = bank()
                for cc in range(4):
                    c = half * 4 + cc
                    tr(sub(pb, pb.ap[:, cc * 128:(cc + 1) * 128]),
                       T(xT[:, c, tt_i * 128:(tt_i + 1) * 128], xT_b[c][tt_i // 4]))
                cp("act" if half == 0 else "dve", sub(ot, ot.ap[:, half * 512:(half + 1) * 512]), pb)
            dma("sp", out_d[sq_i, tt_i * 128:(tt_i + 1) * 128, :], ot.ap, reads=ot.bufs)

    wflush()
    fin = P.add("sp", None)
    fin.deps = [(op, "raw") for op in P.dma_last.values()]
    stats = P.emit(nc, st)
    print("ops/waits per engine", stats)
    st.close()
    return nc


def _blk(Wc):
    K, w = Wc.shape
    return np.ascontiguousarray(Wc.reshape(K // 128, 128, w).transpose(1, 0, 2)).reshape(128, -1)


def pack_weights(w_in, w_gate, w_branch, w_mix_out, w_ff_gate, w_ff_up, w_ff_down):
    Ls = w_in.shape[0]
    out = np.zeros((Ls, 128, W_TOT), np.float32)
    for l in range(Ls):
        wi = w_in[l]
        o = out[l]
        for g in range(3):
            for h in range(4):
                b = W_ATT + (g * 4 + h) * 3072
                cq = (g * 4 + h) * 128
                o[:, b:b + 1024] = _blk(wi[:, O_DQ + cq:O_DQ + cq + 128])
                o[:, b + 1024:b + 2048] = _blk(wi[:, O_DK + cq:O_DK + cq + 128])
                o[:, b + 2048:b + 3072] = _blk(wi[:, O_DV + cq:O_DV + cq + 128])
        for h in range(4):
            cat = np.concatenate([wi[:, O_GQ + h * 64:O_GQ + (h + 1) * 64], wi[:, O_GK + h * 64:O_GK + (h + 1) * 64],
                                  wi[:, O_GV + h * 128:O_GV + (h + 1) * 128], wi[:, O_GO + h * 128:O_GO + (h + 1) * 128]], axis=1)
            o[:, W_GLA + h * 3072:W_GLA + (h + 1) * 3072] = _blk(cat)
        al = np.zeros((D, 32), np.float32)
        al[:, 0:16] = wi[:, O_AL:O_AL + 16]
        o[:, W_ALOW:W_ALOW + 256] = _blk(al)
        for ch in range(4):
            b = W_SC + ch * 3072
            o[:, b:b + 1024] = _blk(wi[:, O_SB + ch * 128:O_SB + (ch + 1) * 128])
            o[:, b + 1024:b + 2048] = _blk(wi[:, O_SC + ch * 128:O_SC + (ch + 1) * 128])
            o[:, b + 2048:b + 3072] = _blk(wi[:, O_SX + ch * 128:O_SX + (ch + 1) * 128])
        for g in range(3):
            for j in range(8):
                b = W_MRG + (g * 8 + j) * 1536
                o[:, b:b + 1024] = _blk(w_gate[l][:, g * D + j * 128:g * D + (j + 1) * 128])
                o[:, b + 1024:b + 1536] = _blk(w_branch[l][g][:, j * 128:(j + 1) * 128])
        for jo in range(8):
            o[:, W_MIX + jo * 1024:W_MIX + (jo + 1) * 1024] = _blk(w_mix_out[l][:, jo * 128:(jo + 1) * 128])
        for f in range(NFF):
            b = W_FF + f * 2048
            o[:, b:b + 1024] = _blk(w_ff_gate[l][:, f * 128:(f + 1) * 128])
            o[:, b + 1024:b + 2048] = _blk(w_ff_up[l][:, f * 128:(f + 1) * 128])
        for j in range(8):
            o[:, W_DN + j * 2816:W_DN + (j + 1) * 2816] = _blk(w_ff_down[l][:, j * 128:(j + 1) * 128])
    return out


def pack_vecs(b_gate, pre_mix_g, post_mix_g, pre_ffn_g, post_ffn_g, ff_conv_w, ff_conv_b, sc_conv_w, gla_norm_g):
    Ls = b_gate.shape[0]
    v = np.zeros((Ls, 128, NV), np.float32)

    def col(a):
        return a.reshape(-1, 128).T

    for l in range(Ls):
        v[l, :, V_PRE_MIX:V_PRE_MIX + 8] = col(pre_mix_g[l])
        v[l, :, V_POST_MIX:V_POST_MIX + 8] = col(post_mix_g[l])
        v[l, :, V_PRE_FFN:V_PRE_FFN + 8] = col(pre_ffn_g[l])
        v[l, :, V_POST_FFN:V_POST_FFN + 8] = col(post_ffn_g[l])
        v[l, :, V_BGATE:V_BGATE + 24] = col(b_gate[l])
        for k in range(3):
            v[l, :, V_FFCW + k * NFF:V_FFCW + (k + 1) * NFF] = col(ff_conv_w[l, k])
            v[l, :, V_SCCW + k * 4:V_SCCW + (k + 1) * 4] = col(sc_conv_w[l, k])
        v[l, :, V_FFCB:V_FFCB + NFF] = col(ff_conv_b[l])
        v[l, :, V_GLAG] = gla_norm_g[l]
    return v


def make_consts():
    c = np.zeros((128, NCONST), np.float32)
    c[:, C_ID:C_ID + 128] = np.eye(128, dtype=np.float32)
    j = np.arange(128)[:, None]
    i = np.arange(128)[None, :]
    c[:, C_TRIL:C_TRIL + 128] = np.where(j <= i, -1.0 / 16.0, 0.0)
    c[:, C_UPP:C_UPP + 128] = np.where(j > i, -1.0 / 16.0, 0.0)
    c[:, C_MASK:C_MASK + 128] = (j <= i)
    c[:, C_MASK + 128:C_MASK + 256] = (j >= i)
    r = np.zeros((128, 128), np.float32)
    for m in range(64):
        r[m + 64, m] = -1.0
        r[m, m + 64] = 1.0
    c[:, C_RMAT:C_RMAT + 128] = r
    inv = (np.float32(10000.0) ** (-(np.arange(0, 128, 2, dtype=np.float32) / np.float32(128.0)))).astype(np.float32)
    c[:, C_INV] = np.concatenate([inv, inv])
    c[:, C_MBIAS:C_MBIAS + 256] = np.where(c[:, C_MASK:C_MASK + 256] > 0.5, 0.0, -30000.0)
    return c


_NC_CACHE = {}


def prepare_shared(w_in, w_alpha_up, b_alpha, gla_norm_g, sc_conv_w, w_gate, b_gate, w_branch, w_mix_out,
                   pre_mix_g, post_mix_g, pre_ffn_g, post_ffn_g, w_ff_gate, w_ff_up, ff_conv_w, ff_conv_b, w_ff_down):
    f = lambda a: np.asarray(a, np.float32)
    wpk = pack_weights(f(w_in), f(w_gate), f(w_branch), f(w_mix_out), f(w_ff_gate), f(w_ff_up), f(w_ff_down))
    vecs = pack_vecs(f(b_gate), f(pre_mix_g), f(post_mix_g), f(pre_ffn_g), f(post_ffn_g), f(ff_conv_w),
                     f(ff_conv_b), f(sc_conv_w), f(gla_norm_g))
    Ls = wpk.shape[0]
    wup = np.zeros((Ls, 32, 512), np.float32)
    wup[:, 0:16, 0:256] = f(w_alpha_up)
    wup[:, 0, 256:512] = f(b_alpha)
    return dict(wpk=wpk, vecs=vecs, wup=wup, consts=make_consts())


def kernel(x, positions, **weights):
    x = np.asarray(x, np.float32)
    positions = np.asarray(positions, np.int32)
    shared = prepare_shared(**weights)
    n_cores = 8
    per = x.shape[0] // n_cores
    if "nc" not in _NC_CACHE:
        _NC_CACHE["nc"] = build(nseq=per, nlayers=2)
    nc = _NC_CACHE["nc"]
    in_maps = []
    for c in range(n_cores):
        xs = np.ascontiguousarray(x[c * per:(c + 1) * per])
        pb = np.ascontiguousarray(np.broadcast_to(positions[c * per:(c + 1) * per, None, :], (per, 128, S)))
        m = dict(x=xs, posb=pb)
        m.update(shared)
        in_maps.append(m)
    res = run_bass_kernel_spmd(nc, in_maps, core_ids=list(range(n_cores)))
    return np.concatenate([r["out"] for r in res.results], axis=0)
```
